# Optimizing a Trainium2 kernel written in Bass

```python
import jax, jax.numpy as jnp
from jax import lax
import numpy as np

D_MODEL = 4096
BATCH = 4
SEQ = 2048
DEPTH = 4

CHUNK = 64
N_MIXERS = 3
ROPE_THETA = 10000.0
NORM_EPS = 1e-6
D_FF = 6144
FFN_RES = 0.5
MLA_HEADS = 32
MLA_Q_LORA = 1024
MLA_KV_LORA = 512
MLA_NOPE = 128
MLA_ROPE = 64
MLA_V = 128
ATTN_Q_BLOCK = 128
DSA_HEADS = 32
DSA_KV_HEADS = 8
DSA_HEAD_DIM = 128
IDX_HEADS = 32
IDX_DIM = 128
IDX_ROPE = 64
DSA_TOPK_MAX = 256
RWKV_HEAD = 64
RWKV_HEADS = D_MODEL // RWKV_HEAD
DECAY_LORA = 128
AAA_LORA = 128
GATE_LORA = 480
GN_EPS = 64e-5

kernel_name = 'hybrid_mla_dsa_rwkv7_macaron'


def rms_norm(x, g):
    xf = x.astype(jnp.float32)
    y = xf * lax.rsqrt(jnp.mean(xf * xf, axis=-1, keepdims=True) + NORM_EPS)
    return (y * g.astype(jnp.float32)).astype(x.dtype)


def rope(x, pos):
    d = x.shape[-1]
    inv = jnp.power(ROPE_THETA, -jnp.arange(0, d, 2, dtype=jnp.float32) / d)
    ang = pos.astype(jnp.float32)[..., None] * inv
    ang = ang.reshape(ang.shape[:2] + (1,) * (x.ndim - 3) + (d // 2,))
    cos, sin = jnp.cos(ang), jnp.sin(ang)
    xf = x.astype(jnp.float32)
    x1, x2 = xf[..., : d // 2], xf[..., d // 2:]
    return jnp.concatenate([x1 * cos - x2 * sin, x2 * cos + x1 * sin], axis=-1).astype(x.dtype)


def swiglu(h, w_gu, w_down):
    gate, up = jnp.split(h @ w_gu, 2, axis=-1)
    return (jax.nn.silu(gate) * up) @ w_down


def mla_mixer(h, pos, w_in, q_norm, w_uq, kv_norm, w_ukv, w_o):
    B, T, _ = h.shape
    lat = h @ w_in
    c_q, c_kv, k_rope = jnp.split(lat, [MLA_Q_LORA, MLA_Q_LORA + MLA_KV_LORA], axis=-1)
    q = (rms_norm(c_q, q_norm) @ w_uq).reshape(B, T, MLA_HEADS, MLA_NOPE + MLA_ROPE)
    q_nope, q_rope = q[..., :MLA_NOPE], rope(q[..., MLA_NOPE:], pos)
    k_rope = rope(k_rope, pos)
    kv = (rms_norm(c_kv, kv_norm) @ w_ukv).reshape(B, T, MLA_HEADS, MLA_NOPE + MLA_V)
    k_nope, v = kv[..., :MLA_NOPE], kv[..., MLA_NOPE:]
    scale = (MLA_NOPE + MLA_ROPE) ** -0.5
    outs = []
    for start in range(0, T, ATTN_Q_BLOCK):
        end = start + ATTN_Q_BLOCK
        s = (jnp.einsum('bqhd,bkhd->bhqk', q_nope[:, start:end], k_nope[:, :end])
             + jnp.einsum('bqhr,bkr->bhqk', q_rope[:, start:end], k_rope[:, :end]))
        s = s.astype(jnp.float32) * scale
        q_chunk = np.arange(start, end) // CHUNK
        k_chunk = np.arange(end) // CHUNK
        s = jnp.where(k_chunk[None, :] <= q_chunk[:, None], s, -jnp.inf)
        p = jax.nn.softmax(s, axis=-1).astype(v.dtype)
        outs.append(jnp.einsum('bhqk,bkhd->bqhd', p, v[:, :end]))
    o = jnp.concatenate(outs, axis=1).reshape(B, T, MLA_HEADS * MLA_V)
    return o @ w_o


def dsa_mixer(h, pos, w_in, w_o):
    B, T, _ = h.shape
    G, HD = DSA_KV_HEADS, DSA_HEAD_DIM
    rep = DSA_HEADS // G
    topk = min(DSA_TOPK_MAX, T // 4)
    sizes = [DSA_HEADS * HD, G * HD, G * HD, IDX_HEADS * IDX_DIM, IDX_DIM]
    splits = [int(s) for s in np.cumsum(sizes)]
    q, k, v, qi, ki, wi = jnp.split(h @ w_in, splits, axis=-1)
    q = rope(q.reshape(B, T, DSA_HEADS, HD), pos)
    k = rope(k.reshape(B, T, G, HD), pos)
    v = v.reshape(B, T, G, HD)
    qi = qi.reshape(B, T, IDX_HEADS, IDX_DIM)
    qi = jnp.concatenate([rope(qi[..., :IDX_ROPE], pos), qi[..., IDX_ROPE:]], axis=-1)
    ki = jnp.concatenate([rope(ki[..., :IDX_ROPE], pos), ki[..., IDX_ROPE:]], axis=-1)
    wi = wi * (IDX_HEADS ** -0.5 * IDX_DIM ** -0.5)
    key_chunk = jnp.arange(T) // CHUNK
    gather = jax.vmap(lambda a, i: a[i])

    def chunk_attn(c):
        st = c * CHUNK
        qi_c = lax.dynamic_slice_in_dim(qi, st, CHUNK, axis=1)
        wi_c = lax.dynamic_slice_in_dim(wi, st, CHUNK, axis=1)
        q_c = lax.dynamic_slice_in_dim(q, st, CHUNK, axis=1)
        dots = jax.nn.relu(jnp.einsum('bqhd,bsd->bqhs', qi_c, ki))
        score = jnp.einsum('bqh,bqhs->bqs', wi_c, dots).astype(jnp.float32)
        score = jnp.where((key_chunk <= c)[None, None, :], score, -jnp.inf)
        _, idx = lax.top_k(score, topk)
        valid = idx < (c + 1) * CHUNK
        k_sel = gather(k, idx)
        v_sel = gather(v, idx)
        qg = q_c.reshape(B, CHUNK, G, rep, HD)
        s = jnp.einsum('bqgrd,bqkgd->bqgrk', qg, k_sel).astype(jnp.float32) * HD ** -0.5
        s = jnp.where(valid[:, :, None, None, :], s, -jnp.inf)
        p = jax.nn.softmax(s, axis=-1).astype(v.dtype)
        o = jnp.einsum('bqgrk,bqkgd->bqgrd', p, v_sel)
        return o.reshape(B, CHUNK, DSA_HEADS * HD)

    o = lax.map(chunk_attn, jnp.arange(T // CHUNK))
    o = jnp.moveaxis(o, 0, 1).reshape(B, T, DSA_HEADS * HD)
    return o @ w_o


def wkv7_scan(r, w, k, v, a, b):
    B, T, H, N = r.shape

    def step(S, inp):
        r_t, w_t, k_t, v_t, a_t, b_t = inp
        sa = jnp.einsum('bhvk,bhk->bhv', S, a_t)
        S = (S * w_t[:, :, None, :] + sa[..., None] * b_t[:, :, None, :]
             + v_t[..., None] * k_t[:, :, None, :])
        return S, jnp.einsum('bhvk,bhk->bhv', S, r_t)

    xs = tuple(jnp.moveaxis(t.astype(jnp.float32), 1, 0) for t in (r, w, k, v, a, b))
    S0 = jnp.zeros((B, H, N, N), jnp.float32)
    _, y = lax.scan(step, S0, xs)
    return jnp.moveaxis(y, 0, 1)


def rwkv7_mixer(h, mu, w_r, w_k, w_v, w_o, w0, w1, w2, a0, a1, a2, g1, g2,
                k_k, k_a, r_k, lnx_w, lnx_b):
    B, T, C = h.shape
    H, N = RWKV_HEADS, RWKV_HEAD
    heads = lambda t: t.reshape(B, T, H, N)
    xx = jnp.pad(h, ((0, 0), (1, 0), (0, 0)))[:, :-1] - h
    mu_r, mu_w, mu_k, mu_v, mu_a, mu_g = mu
    xr, xw, xk = h + xx * mu_r, h + xx * mu_w, h + xx * mu_k
    xv, xa, xg = h + xx * mu_v, h + xx * mu_a, h + xx * mu_g
    r = xr @ w_r
    w_log = -jax.nn.softplus(-(w0 + jnp.tanh(xw @ w1) @ w2)) - 0.5
    k = xk @ w_k
    v = xv @ w_v
    a = jax.nn.sigmoid(a0 + (xa @ a1) @ a2)
    g = jax.nn.sigmoid(xg @ g1) @ g2
    kk = heads(k * k_k).astype(jnp.float32)
    kk = kk / jnp.maximum(jnp.sqrt(jnp.sum(kk * kk, axis=-1, keepdims=True)), 1e-12)
    k = k * (1 + (a - 1) * k_a)
    decay = jnp.exp(-jnp.exp(w_log.astype(jnp.float32)))
    y = wkv7_scan(heads(r), heads(decay), heads(k), heads(v), -kk,
                  kk * heads(a).astype(jnp.float32))
    mean = jnp.mean(y, axis=-1, keepdims=True)
    var = jnp.mean(jnp.square(y - mean), axis=-1, keepdims=True)
    y = ((y - mean) * lax.rsqrt(var + GN_EPS)).reshape(B, T, C) * lnx_w + lnx_b
    bonus = jnp.sum(heads(r) * heads(k) * r_k, axis=-1, keepdims=True) * heads(v)
    y = y + bonus.reshape(B, T, C)
    return (y * g).astype(h.dtype) @ w_o


def setup_inputs(seed: int = 0) -> dict:
    key = jax.random.key(seed)
    ks = list(jax.random.split(key, 96))
    D, F = D_MODEL, D_FF

    def nrm(shape, scale):
        return jax.random.normal(ks.pop(), shape, jnp.float32) * scale

    def gain(shape):
        return 1.0 + nrm(shape, 0.05)

    def ffn_params(i):
        return {'ffn_norm_%d' % i: gain((2, D)),
                'ffn_w_gu_%d' % i: nrm((2, D, 2 * F), D ** -0.5),
                'ffn_w_down_%d' % i: nrm((2, F, D), F ** -0.5),
                'mix_norm_%d' % i: gain((D,))}

    def mla_params(i):
        p = 'mla%d_' % i
        return {p + 'w_in': nrm((D, MLA_Q_LORA + MLA_KV_LORA + MLA_ROPE), D ** -0.5),
                p + 'q_norm': gain((MLA_Q_LORA,)),
                p + 'w_uq': nrm((MLA_Q_LORA, MLA_HEADS * (MLA_NOPE + MLA_ROPE)), MLA_Q_LORA ** -0.5),
                p + 'kv_norm': gain((MLA_KV_LORA,)),
                p + 'w_ukv': nrm((MLA_KV_LORA, MLA_HEADS * (MLA_NOPE + MLA_V)), MLA_KV_LORA ** -0.5),
                p + 'w_o': nrm((MLA_HEADS * MLA_V, D), (MLA_HEADS * MLA_V) ** -0.5)}

    def dsa_params(i):
        p = 'dsa%d_' % i
        width = (DSA_HEADS * DSA_HEAD_DIM + 2 * DSA_KV_HEADS * DSA_HEAD_DIM
                 + IDX_HEADS * IDX_DIM + IDX_DIM + IDX_HEADS)
        return {p + 'w_in': nrm((D, width), D ** -0.5),
                p + 'w_o': nrm((DSA_HEADS * DSA_HEAD_DIM, D), (DSA_HEADS * DSA_HEAD_DIM) ** -0.5)}

    def rwkv_params(i):
        p = 'rwkv%d_' % i
        return {p + 'mu': jax.random.uniform(ks.pop(), (6, D), jnp.float32),
                p + 'w_r': nrm((D, D), D ** -0.5),
                p + 'w_k': nrm((D, D), D ** -0.5),
                p + 'w_v': nrm((D, D), D ** -0.5),
                p + 'w_o': nrm((D, D), D ** -0.5),
                p + 'w0': -1.0 + nrm((D,), 0.5),
                p + 'w1': nrm((D, DECAY_LORA), D ** -0.5),
                p + 'w2': nrm((DECAY_LORA, D), 0.1 * DECAY_LORA ** -0.5),
                p + 'a0': nrm((D,), 0.1),
                p + 'a1': nrm((D, AAA_LORA), D ** -0.5),
                p + 'a2': nrm((AAA_LORA, D), 0.1 * AAA_LORA ** -0.5),
                p + 'g1': nrm((D, GATE_LORA), D ** -0.5),
                p + 'g2': nrm((GATE_LORA, D), GATE_LORA ** -0.5),
                p + 'k_k': 0.85 + nrm((D,), 0.05),
                p + 'k_a': gain((D,)),
                p + 'r_k': nrm((RWKV_HEADS, RWKV_HEAD), 0.1),
                p + 'lnx_w': gain((D,)),
                p + 'lnx_b': nrm((D,), 0.01)}

    x = jax.random.normal(ks.pop(), (BATCH, SEQ, D), jnp.float32)
    start = jax.random.randint(ks.pop(), (BATCH, 1), 0, 4096, dtype=jnp.int32)
    positions = start + jnp.arange(SEQ, dtype=jnp.int32)[None, :]
    inputs = {'x': x, 'positions': positions}
    inputs.update(ffn_params(0)); inputs.update(mla_params(0))
    inputs.update(ffn_params(1)); inputs.update(dsa_params(1))
    inputs.update(ffn_params(2)); inputs.update(rwkv_params(2))
    inputs.update(ffn_params(3)); inputs.update(mla_params(3))
    inputs['final_norm'] = gain((D,))
    return inputs


def reference(x, positions,
              ffn_norm_0, ffn_w_gu_0, ffn_w_down_0, mix_norm_0,
              mla0_w_in, mla0_q_norm, mla0_w_uq, mla0_kv_norm, mla0_w_ukv, mla0_w_o,
              ffn_norm_1, ffn_w_gu_1, ffn_w_down_1, mix_norm_1,
              dsa1_w_in, dsa1_w_o,
              ffn_norm_2, ffn_w_gu_2, ffn_w_down_2, mix_norm_2,
              rwkv2_mu, rwkv2_w_r, rwkv2_w_k, rwkv2_w_v, rwkv2_w_o, rwkv2_w0, rwkv2_w1, rwkv2_w2,
              rwkv2_a0, rwkv2_a1, rwkv2_a2, rwkv2_g1, rwkv2_g2, rwkv2_k_k, rwkv2_k_a, rwkv2_r_k,
              rwkv2_lnx_w, rwkv2_lnx_b,
              ffn_norm_3, ffn_w_gu_3, ffn_w_down_3, mix_norm_3,
              mla3_w_in, mla3_q_norm, mla3_w_uq, mla3_kv_norm, mla3_w_ukv, mla3_w_o,
              final_norm):
    mixers = [
        lambda h: mla_mixer(h, positions, mla0_w_in, mla0_q_norm, mla0_w_uq,
                            mla0_kv_norm, mla0_w_ukv, mla0_w_o),
        lambda h: dsa_mixer(h, positions, dsa1_w_in, dsa1_w_o),
        lambda h: rwkv7_mixer(h, rwkv2_mu, rwkv2_w_r, rwkv2_w_k, rwkv2_w_v, rwkv2_w_o,
                              rwkv2_w0, rwkv2_w1, rwkv2_w2, rwkv2_a0, rwkv2_a1, rwkv2_a2,
                              rwkv2_g1, rwkv2_g2, rwkv2_k_k, rwkv2_k_a, rwkv2_r_k,
                              rwkv2_lnx_w, rwkv2_lnx_b),
        lambda h: mla_mixer(h, positions, mla3_w_in, mla3_q_norm, mla3_w_uq,
                            mla3_kv_norm, mla3_w_ukv, mla3_w_o),
    ]
    ffns = [(ffn_norm_0, ffn_w_gu_0, ffn_w_down_0, mix_norm_0),
            (ffn_norm_1, ffn_w_gu_1, ffn_w_down_1, mix_norm_1),
            (ffn_norm_2, ffn_w_gu_2, ffn_w_down_2, mix_norm_2),
            (ffn_norm_3, ffn_w_gu_3, ffn_w_down_3, mix_norm_3)]
    for i in range(DEPTH):
        f_norm, w_gu, w_down, m_norm = ffns[i]
        x = x + FFN_RES * swiglu(rms_norm(x, f_norm[0]), w_gu[0], w_down[0])
        x = x + mixers[i](rms_norm(x, m_norm))
        x = x + FFN_RES * swiglu(rms_norm(x, f_norm[1]), w_gu[1], w_down[1])
    return rms_norm(x, final_norm)
```

```python
import math
from contextlib import ExitStack
import numpy as np
import ml_dtypes
import concourse.bass as bass
import concourse.mybir as mybir
from concourse.bass_utils import run_bass_kernel_spmd

F32 = mybir.dt.float32
BF16 = mybir.dt.bfloat16
I32 = mybir.dt.int32
ALU = mybir.AluOpType
AF = mybir.ActivationFunctionType
AX = mybir.AxisListType

NORM_EPS = 1e-6
EPOCH = 12000
NEG = -30000.0
_UN = [0]


def un(n):
    _UN[0] += 1
    return "%s__%d" % (n, _UN[0])


class Op:
    __slots__ = ("eng", "fn", "deps", "signaled", "count", "is_dma", "dsem", "dval")

    def __init__(self, eng, fn):
        self.eng = eng
        self.fn = fn
        self.deps = []
        self.signaled = False
        self.is_dma = False
        self.dsem = None
        self.dval = 0
        self.count = None


class Buf:
    __slots__ = ("w", "r")

    def __init__(self):
        self.w = None
        self.r = []


class Prog:
    ENGS = ("pe", "act", "dve", "pool", "sp")

    def __init__(self, nc):
        self.nc = nc
        self.ops = {e: [] for e in self.ENGS}
        self.bufs = {}
        self.dma_sems = {}

    def buf(self, name):
        b = self.bufs.get(name)
        if b is None:
            b = self.bufs[name] = Buf()
        return b

    def _collect(self, op, reads, writes):
        deps = []
        for n in reads:
            b = self.buf(n)
            if b.w is not None:
                deps.append(b.w)
        for n in writes:
            b = self.buf(n)
            if b.w is not None:
                deps.append(b.w)
            deps.extend(b.r)
        for d in deps:
            if d is op:
                continue
            if d.is_dma or d.eng != op.eng or op.eng != "pe":
                op.deps.append(d)
                if not d.is_dma:
                    d.signaled = True
        for n in reads:
            b = self.buf(n)
            if op.is_dma:
                b.r.append(op)
            else:
                b.r = [o for o in b.r if o.is_dma or o.eng != op.eng]
                b.r.append(op)
        for n in writes:
            b = self.buf(n)
            b.w = op
            b.r = []

    def emit(self, eng, fn, reads=(), writes=()):
        op = Op(eng, fn)
        self._collect(op, reads, writes)
        self.ops[eng].append(op)
        return op

    def dma(self, eng, out, in_, semkey, reads=(), writes=(), grp=None, **kw):
        ent = self.dma_sems.get(semkey)
        if ent is None:
            ent = self.dma_sems[semkey] = [self.nc.alloc_semaphore(name="d_" + semkey), 0, None, None, []]
        sem = ent[0]
        ent[1] += 16
        val = ent[1]

        def fn(e, out=out, in_=in_, sem=sem, kw=kw):
            return e.dma_start(out=out, in_=in_, **kw).then_inc(sem, 16)

        op = Op(eng, fn)
        op.is_dma = True
        op.dsem = sem
        op.dval = val
        same = grp is not None and ent[3] == grp
        if ent[2] is not None and not same:
            op.deps.append(ent[2])
        if same:
            for o in ent[4]:
                o.dval = val
            ent[4].append(op)
        else:
            ent[3] = grp
            ent[4] = [op]
        self._collect(op, reads, writes)
        ent[2] = op
        self.ops[eng].append(op)
        return op

    def barrier(self):
        lasts = []
        for e in self.ENGS:
            for o in reversed(self.ops[e]):
                if not o.is_dma and o.fn is not None:
                    lasts.append(o)
                    break
        dmas = [ent[2] for ent in self.dma_sems.values() if ent[2] is not None]
        for e in self.ENGS:
            op = Op(e, None)
            for d in lasts:
                if d.eng != e:
                    op.deps.append(d)
                    d.signaled = True
            op.deps.extend(dmas)
            self.ops[e].append(op)
        for b in self.bufs.values():
            b.w = None
            b.r = []

    def finalize(self):
        nc = self.nc
        sems = {e: [] for e in self.ENGS}
        for e in self.ENGS:
            c = 0
            for op in self.ops[e]:
                if op.is_dma or op.fn is None:
                    continue
                if op.signaled:
                    c += 1
                    ep = (c - 1) // EPOCH
                    while len(sems[e]) <= ep:
                        sems[e].append(nc.alloc_semaphore(name="s_%s_%d" % (e, len(sems[e]))))
                    op.count = (sems[e][ep], c - ep * EPOCH)
        hmap = {"pe": "tensor", "act": "scalar", "dve": "vector", "pool": "gpsimd", "sp": "sync"}
        stats = {}
        with nc.Block() as block:
            for e in self.ENGS:
                ops = self.ops[e]

                def body(h, ops=ops, e=e):
                    known = {}
                    nw = 0
                    for op in ops:
                        need = {}
                        for d in op.deps:
                            if d.is_dma:
                                s, v = d.dsem, d.dval
                            else:
                                s, v = d.count
                            k = s.num
                            if known.get(k, 0) >= v:
                                continue
                            if k not in need or need[k][1] < v:
                                need[k] = (s, v)
                        for k, (s, v) in need.items():
                            known[k] = v
                            h.wait_ge(s, v)
                            nw += 1
                        if op.fn is None:
                            continue
                        ins = op.fn(h)
                        if (not op.is_dma) and op.signaled:
                            ins.then_inc(op.count[0], 1)
                    stats[e] = (len(ops), nw)

                getattr(block, hmap[e])(body)
        return stats


class Ctx:
    def __init__(self, nc, D, TOK, consts_ap):
        self.nc = nc
        self.P = Prog(nc)
        self.D = D
        self.DC = D // 128
        self.TOK = TOK
        self.NT = TOK // 512
        self.cnt = {}
        P = self.P
        self.psS = nc.alloc_psum_tensor("psS", [128, 2048], F32)
        self.psum = [self.psS[:, i * 512:(i + 1) * 512] for i in range(4)]
        self.psum += [nc.alloc_psum_tensor("ps%d" % i, [128, 512], F32)[:] for i in range(4, 8)]
        self.ones = nc.alloc_sbuf_tensor("ones", [128, 128], F32)
        self.eps = nc.alloc_sbuf_tensor("eps", [128, 1], F32)
        self.cf = nc.alloc_sbuf_tensor("cf", [128, 3, 128], F32)
        self.identb = nc.alloc_sbuf_tensor("identb", [128, 128], BF16)
        P.emit("pool", lambda e: e.memset(self.ones[:], 1.0), writes=["ones"])
        P.emit("pool", lambda e: e.memset(self.eps[:], NORM_EPS), writes=["eps"])
        P.dma("sp", self.cf[:], consts_ap, "cf", writes=["cf"])
        P.emit("dve", lambda e: e.tensor_copy(out=self.identb[:], in_=self.cf[:, 0, :]), reads=["cf"], writes=["identb"])

    def basics(self, es, wcols=8192, nw=2):
        nc, TOK = self.nc, self.TOK
        self.wbuf = [es.enter_context(nc.sbuf_tensor(un("w%d" % i), [128, wcols], BF16)) for i in range(nw)]
        self.nw = nw
        self.xc = [es.enter_context(nc.sbuf_tensor(un("xc%d" % i), [128, TOK], F32)) for i in range(2)]
        self.sg = [es.enter_context(nc.sbuf_tensor(un("sg%d" % i), [128, TOK], F32)) for i in range(2)]
        self.hout = [es.enter_context(nc.sbuf_tensor(un("hout%d" % i), [128, TOK], BF16)) for i in range(2)]
        self.rstd = es.enter_context(nc.sbuf_tensor(un("rstd"), [128, TOK], F32))

    def rot(self, key, n):
        v = self.cnt.get(key, 0)
        self.cnt[key] = v + 1
        return v % n

    def load_w(self, dram_ap, ncols, shape3=None):
        s = self.rot("w", self.nw)
        dst = self.wbuf[s][:, 0:ncols]
        if shape3 is not None:
            dst = dst.rearrange(shape3[0], **shape3[1])
        self.P.dma("pool", dst, dram_ap, "w%d" % s, writes=["w%d" % s])
        return self.wbuf[s], "w%d" % s


def dview(ap):
    return ap.rearrange("(c p) t -> c p t", p=128)


def emit_stats(cx, src_v, nchunks, inv_n, dkey, load=True, src_sb=None):
    xc_, sg_, rstd_, hout_ = cx.xc, cx.sg, cx.rstd, cx.hout
    P, TOK, NT = cx.P, cx.TOK, cx.NT
    pb = cx.rot("pb", 2) * 4
    for kc in range(nchunks):
        if src_sb is None:
            s = cx.rot("xc", 2)
            P.dma("sp", xc_[s][:], src_v[kc], "xc%d" % s, reads=["%s_%d" % (dkey, kc)], writes=["xc%d" % s])
            xin, xname = xc_[s][:], "xc%d" % s
        else:
            xin, xname = src_sb(kc)
        q = cx.rot("sg", 2)
        P.emit("act", lambda e, q=q, xin=xin: e.activation(out=sg_[q][:, 0:TOK], in_=xin, func=AF.Square),
               reads=[xname], writes=["sg%d" % q])
        for th in range(NT):
            P.emit("pe", lambda e, q=q, th=th, kc=kc: e.matmul(cx.psum[pb + th][:], cx.ones[:], sg_[q][:, th * 512:(th + 1) * 512],
                                                           start=(kc == 0), stop=(kc == nchunks - 1)),
                   reads=["sg%d" % q, "ones"], writes=["ps%d" % (pb + th)])
    for th in range(NT):
        P.emit("act", lambda e, th=th: e.activation(out=rstd_[:, th * 512:(th + 1) * 512], in_=cx.psum[pb + th][:], func=AF.Sqrt,
                                                    bias=cx.eps[:, 0:1], scale=inv_n),
               reads=["ps%d" % (pb + th), "eps"], writes=["rstd"])
    P.emit("dve", lambda e: e.reciprocal(out=rstd_[:, 0:TOK], in_=rstd_[:, 0:TOK]), reads=["rstd"], writes=["rstd"])


def emit_norm(cx, src_ap, dkey, gain, gname, hT=None, out_dram=None, out_dt=BF16, okey=None):
    xc_, sg_, rstd_, hout_ = cx.xc, cx.sg, cx.rstd, cx.hout
    P, TOK, DC = cx.P, cx.TOK, cx.DC
    src_v = dview(src_ap)
    emit_stats(cx, src_v, DC, 1.0 / cx.D, dkey)
    for kc in range(DC):
        s = cx.rot("xc", 2)
        P.dma("sp", xc_[s][:], src_v[kc], "xc%d" % s, reads=["%s_%d" % (dkey, kc)], writes=["xc%d" % s])
        if hT is not None:
            P.emit("dve", lambda e, s=s, kc=kc: e.scalar_tensor_tensor(out=hT[:, kc, :], in0=xc_[s][:], scalar=gain[:, kc:kc + 1],
                                                                     in1=rstd_[:, 0:TOK], op0=ALU.mult, op1=ALU.mult),
                   reads=["xc%d" % s, "rstd", gname], writes=["hT_%d" % kc])
        if out_dram is not None:
            ov = dview(out_dram)
            q = cx.rot("sg", 2)
            if out_dt == BF16:
                dst = hout_[q][:, 0:TOK]
                dname = "hout%d" % q
            else:
                dst = sg_[q][:, 0:TOK]
                dname = "sg%d" % q
            P.emit("dve", lambda e, s=s, kc=kc, dst=dst: e.scalar_tensor_tensor(out=dst, in0=xc_[s][:], scalar=gain[:, kc:kc + 1],
                                                                              in1=rstd_[:, 0:TOK], op0=ALU.mult, op1=ALU.mult),
                   reads=["xc%d" % s, "rstd", gname], writes=[dname])
            P.dma("sp", ov[kc], dst, "ho%d" % q, reads=[dname], writes=["%s_%d" % (okey, kc)])


def emit_down(cx, aT, aname, KC, wslab, src_ap, skey, dst_ap, dkey, scale):
    xc_, sg_, rstd_, hout_ = cx.xc, cx.sg, cx.rstd, cx.hout
    P, TOK, NT, DC = cx.P, cx.TOK, cx.NT, cx.DC
    sv, dv = dview(src_ap), dview(dst_ap)
    for i in range(DC):
        wt, wname = cx.load_w(wslab(i), KC * 128)
        s = cx.rot("xc", 2)
        P.dma("sp", xc_[s][:], sv[i], "xc%d" % s, reads=["%s_%d" % (skey, i)], writes=["xc%d" % s])
        pb = cx.rot("pb", 2) * 4
        for fc in range(KC):
            for th in range(NT):
                P.emit("pe", lambda e, wt=wt, th=th, fc=fc, pb=pb: e.matmul(
                    cx.psum[pb + th][:], wt[:, fc * 128:(fc + 1) * 128], aT[:, fc, th * 512:(th + 1) * 512],
                    start=(fc == 0), stop=(fc == KC - 1)),
                    reads=[wname, "%s_%d" % (aname, fc)], writes=["ps%d" % (pb + th)])
        for th in range(NT):
            P.emit("dve", lambda e, s=s, th=th, pb=pb: e.scalar_tensor_tensor(
                out=xc_[s][:, th * 512:(th + 1) * 512], in0=cx.psum[pb + th][:], scalar=scale, in1=xc_[s][:, th * 512:(th + 1) * 512],
                op0=ALU.mult, op1=ALU.add), reads=["ps%d" % (pb + th), "xc%d" % s], writes=["xc%d" % s])
        P.dma("sp", dv[i], xc_[s][:], "xo%d" % s, reads=["xc%d" % s], writes=["%s_%d" % (dkey, i)])


def emit_ffn(cx, src_ap, skey, dst_ap, dkey, wgu, wd, gain, gname, hT, aT, F, NSPLIT):
    xc_, sg_, rstd_, hout_ = cx.xc, cx.sg, cx.rstd, cx.hout
    P, TOK, NT, DC = cx.P, cx.TOK, cx.NT, cx.DC
    FB = F // 128
    FH = FB // NSPLIT
    emit_norm(cx, src_ap, skey, gain, gname, hT=hT)
    for sp in range(NSPLIT):
        for jj in range(FH):
            j = sp * FH + jj
            wt, wname = cx.load_w(wgu[j].rearrange("t p n -> p t n"), 2 * DC * 128, ("p (t n) -> p t n", dict(t=2)))
            pb = cx.rot("pb", 2) * 4
            for kc in range(DC):
                for g in range(2):
                    for th in range(NT):
                        P.emit("pe", lambda e, wt=wt, g=g, th=th, kc=kc, pb=pb: e.matmul(
                            cx.psum[pb + g * 2 + th][:], wt[:, (g * DC + kc) * 128:(g * DC + kc + 1) * 128],
                            hT[:, kc, th * 512:(th + 1) * 512], start=(kc == 0), stop=(kc == DC - 1)),
                            reads=[wname, "hT_%d" % kc], writes=["ps%d" % (pb + g * 2 + th)])
            for th in range(NT):
                q = cx.rot("sg", 2)
                P.emit("act", lambda e, q=q, th=th, pb=pb: e.activation(out=sg_[q][:, 0:512], in_=cx.psum[pb + th][:], func=AF.Silu),
                       reads=["ps%d" % (pb + th)], writes=["sg%d" % q])
                P.emit("dve", lambda e, q=q, th=th, pb=pb, jj=jj: e.tensor_tensor(
                    out=aT[:, jj, th * 512:(th + 1) * 512], in0=cx.psum[pb + 2 + th][:], in1=sg_[q][:, 0:512], op=ALU.mult),
                    reads=["ps%d" % (pb + 2 + th), "sg%d" % q], writes=["aT_%d" % jj])
        emit_down(cx, aT, "aT", FH, lambda i, sp=sp: wd[sp, i], src_ap if sp == 0 else dst_ap, skey if sp == 0 else dkey,
                  dst_ap, dkey, 0.5)


TWO_PI = 2.0 * math.pi


def emit_rope_tables(cx, pos_ap, n, invf, sgn, cosT, sinT, tname):
    P, nc = cx.P, cx.nc
    with ExitStack() as es:
        posi = es.enter_context(nc.sbuf_tensor(un("rt_posi"), [128, n], I32))
        ang = es.enter_context(nc.sbuf_tensor(un("rt_ang"), [128, n], F32))
        ki = es.enter_context(nc.sbuf_tensor(un("rt_ki"), [128, n], I32))
        tf = es.enter_context(nc.sbuf_tensor(un("rt_tf"), [128, n], F32))
        P.dma("sp", posi[:], pos_ap.to_broadcast([128, n]), "rtp", writes=["rt_posi"])
        P.emit("dve", lambda e: e.tensor_copy(out=ang[:], in_=posi[:]), reads=["rt_posi"], writes=["rt_ang"])
        P.emit("dve", lambda e: e.tensor_scalar(out=ang[:], in0=ang[:], scalar1=invf, scalar2=None, op0=ALU.mult),
               reads=["rt_ang", "vecs"], writes=["rt_ang"])
        for phase, dst, dname, sc in ((0.0, sinT, tname + "_sin", sgn), (0.5 * math.pi, cosT, tname + "_cos", 1.0)):
            P.emit("dve", lambda e, phase=phase: e.tensor_scalar(out=ki[:], in0=ang[:], scalar1=1.0 / TWO_PI, scalar2=phase / TWO_PI,
                                                                op0=ALU.mult, op1=ALU.add), reads=["rt_ang"], writes=["rt_ki"])
            P.emit("dve", lambda e: e.tensor_copy(out=tf[:], in_=ki[:]), reads=["rt_ki"], writes=["rt_tf"])
            P.emit("dve", lambda e: e.scalar_tensor_tensor(out=tf[:], in0=tf[:], scalar=-TWO_PI, in1=ang[:], op0=ALU.mult, op1=ALU.add),
                   reads=["rt_tf", "rt_ang"], writes=["rt_tf"])
            P.emit("dve", lambda e, phase=phase: e.tensor_scalar(out=tf[:], in0=tf[:], scalar1=phase, scalar2=math.pi, op0=ALU.add, op1=ALU.min),
                   reads=["rt_tf"], writes=["rt_tf"])
            P.emit("dve", lambda e: e.tensor_scalar(out=tf[:], in0=tf[:], scalar1=-math.pi, scalar2=None, op0=ALU.max),
                   reads=["rt_tf"], writes=["rt_tf"])
            P.emit("act", lambda e, dst=dst, sc=sc: e.activation(out=dst[:, 0:n], in_=tf[:], func=AF.Sin, scale=sc),
                   reads=["rt_tf", "vecs"], writes=[dname])
        P.barrier()


def emit_rope(cx, src_ps, sname, R, n, cos_ap, sin_ap, tnames, perm, dst_ap, dname):
    P = cx.P
    q = cx.rot("ropet", 2)
    ra, rb, rc = cx.ra[q], cx.rb[q], cx.rc[q]
    pz = 4 + cx.rot("pb2", 2)
    P.emit("act", lambda e: e.activation(out=ra[0:R, 0:n], in_=src_ps, func=AF.Copy), reads=[sname], writes=["ra%d" % q])
    P.emit("pe", lambda e: e.matmul(cx.psum[pz][0:R, 0:n], perm, ra[0:R, 0:n], start=True, stop=True),
           reads=["ra%d" % q, "cf"], writes=["ps%d" % pz])
    P.emit("dve", lambda e: e.tensor_tensor(out=rb[0:R, 0:n], in0=ra[0:R, 0:n], in1=cos_ap, op=ALU.mult),
           reads=["ra%d" % q, tnames[0]], writes=["rb%d" % q])
    P.emit("dve", lambda e: e.tensor_tensor(out=rc[0:R, 0:n], in0=cx.psum[pz][0:R, 0:n], in1=sin_ap, op=ALU.mult),
           reads=["ps%d" % pz, tnames[1]], writes=["rc%d" % q])
    P.emit("dve", lambda e: e.tensor_tensor(out=dst_ap, in0=rb[0:R, 0:n], in1=rc[0:R, 0:n], op=ALU.add),
           reads=["rb%d" % q, "rc%d" % q], writes=[dname])


def alloc_rope_tmps(cx, es):
    nc = cx.nc
    cx.ra = [es.enter_context(nc.sbuf_tensor(un("ra%d" % i), [128, 512], F32)) for i in range(2)]
    cx.rb = [es.enter_context(nc.sbuf_tensor(un("rb%d" % i), [128, 512], F32)) for i in range(2)]
    cx.rc = [es.enter_context(nc.sbuf_tensor(un("rc%d" % i), [128, 512], F32)) for i in range(2)]


def alloc_attn(cx, es, TT):
    nc = cx.nc
    cx.pexp = [es.enter_context(nc.sbuf_tensor(un("pexp%d" % i), [128, TT], BF16)) for i in range(2)]
    cx.PT = [es.enter_context(nc.sbuf_tensor(un("PT%d" % i), [128, TT], BF16)) for i in range(2)]
    cx.mk = [es.enter_context(nc.sbuf_tensor(un("mk%d" % i), [128, TT], BF16)) for i in range(2)]
    cx.otm = [es.enter_context(nc.sbuf_tensor(un("otm%d" % i), [128, 128], BF16)) for i in range(2)]
    cx.sm = [es.enter_context(nc.sbuf_tensor(un("sm%d" % i), [128, 4], F32)) for i in range(2)]


def emit_attn_tile(cx, TT, qk_pairs, vfn, mask_ap, mname, scale, odst, oname):
    P = cx.P
    NKC = TT // 512
    for c in range(NKC):
        for idx, (l, ln, r, rn) in enumerate(qk_pairs(c)):
            P.emit("pe", lambda e, c=c, l=l, r=r, idx=idx: e.matmul(cx.psum[c][:], l, r, start=(idx == 0), stop=False),
                   reads=[ln, rn], writes=["ps%d" % c])
        P.emit("pe", lambda e, c=c: e.matmul(cx.psum[c][:], cx.identb[:], mask_ap[:, c * 512:(c + 1) * 512], start=False, stop=True),
               reads=["identb", mname], writes=["ps%d" % c])
    sname = ["ps%d" % c for c in range(NKC)]
    q = cx.rot("attn", 2)
    sm, pexp, PT, otm = cx.sm[q], cx.pexp[q], cx.PT[q], cx.otm[q]
    P.emit("dve", lambda e: e.tensor_reduce(out=sm[:, 0:1], in_=cx.psS[:, 0:TT], axis=AX.X, op=ALU.max), reads=sname, writes=["sm%d" % q])
    P.emit("dve", lambda e: e.tensor_scalar(out=sm[:, 1:2], in0=sm[:, 0:1], scalar1=-scale, scalar2=None, op0=ALU.mult),
           reads=["sm%d" % q], writes=["sm%d" % q])
    P.emit("act", lambda e: e.activation(out=pexp[:, 0:TT], in_=cx.psS[:, 0:TT], func=AF.Exp, bias=sm[:, 1:2], scale=scale,
                                         accum_out=sm[:, 2:3]), reads=sname + ["sm%d" % q], writes=["pexp%d" % q, "sm%d" % q])
    P.emit("dve", lambda e: e.reciprocal(out=sm[:, 3:4], in_=sm[:, 2:3]), reads=["sm%d" % q], writes=["sm%d" % q])
    ps6b = cx.psum[6].bitcast(BF16)
    for g in range(NKC):
        hf = cx.rot("pt", 2)
        for j in range(4):
            kb = g * 4 + j
            P.emit("pe", lambda e, hf=hf, j=j, kb=kb: e.transpose(out=ps6b[:, hf * 512 + j * 128: hf * 512 + (j + 1) * 128],
                                                                 in_=pexp[:, kb * 128:(kb + 1) * 128], identity=cx.identb[:]),
                   reads=["pexp%d" % q, "identb"], writes=["ps6"])
        if g % 2 == 0:
            P.emit("act", lambda e, hf=hf, g=g: e.activation(out=PT[:, g * 512:(g + 1) * 512], in_=ps6b[:, hf * 512:(hf + 1) * 512], func=AF.Copy),
                   reads=["ps6"], writes=["PT%d_%d" % (q, g)])
        else:
            P.emit("dve", lambda e, hf=hf, g=g: e.tensor_copy(out=PT[:, g * 512:(g + 1) * 512], in_=ps6b[:, hf * 512:(hf + 1) * 512]),
                   reads=["ps6"], writes=["PT%d_%d" % (q, g)])
    nkb = TT // 128
    for kb in range(nkb):
        v, vn = vfn(kb)
        P.emit("pe", lambda e, kb=kb, v=v: e.matmul(cx.psum[7][:, 0:128], PT[:, kb * 128:(kb + 1) * 128], v, start=(kb == 0), stop=(kb == nkb - 1)),
               reads=["PT%d_%d" % (q, kb // 4), vn], writes=["ps7"])
    P.emit("act", lambda e: e.activation(out=otm[:], in_=cx.psum[7][:, 0:128], func=AF.Copy, scale=sm[:, 3:4]),
           reads=["ps7", "sm%d" % q], writes=["otm%d" % q])
    ps7b = cx.psum[7].bitcast(BF16)
    P.emit("pe", lambda e: e.transpose(out=ps7b[:, 512:640], in_=otm[:], identity=cx.identb[:]), reads=["otm%d" % q, "identb"], writes=["ps7"])
    P.emit("dve", lambda e: e.tensor_copy(out=odst, in_=ps7b[:, 512:640]), reads=["ps7"], writes=[oname])


def emit_proj(cx, ps_ap, psname, KC, lhs_fn, rhs_fn):
    for kc in range(KC):
        l, ln = lhs_fn(kc)
        r, rn = rhs_fn(kc)
        cx.P.emit("pe", lambda e, l=l, r=r, kc=kc: e.matmul(ps_ap, l, r, start=(kc == 0), stop=(kc == KC - 1)),
                  reads=[ln, rn], writes=[psname])


def emit_evac(cx, dst, dname, src, sname, k):
    if k % 2 == 0:
        cx.P.emit("act", lambda e: e.activation(out=dst, in_=src, func=AF.Copy), reads=[sname], writes=[dname])
    else:
        cx.P.emit("dve", lambda e: e.tensor_copy(out=dst, in_=src), reads=[sname], writes=[dname])


def load_hTc(cx, hTc, src_ap, c0, W=512):
    v = src_ap.rearrange("(c p) t -> p c t", p=128)
    cx.P.dma("sp", hTc[:, :, 0:W], v[:, :, c0:c0 + W], "hTc", writes=["hTc"])


def emit_mla(cx, io, MH, QL, KVL, TT):
    P, nc, TOK, DC = cx.P, cx.nc, cx.TOK, cx.DC
    QC, KVC = QL // 128, KVL // 128
    NQT = TOK // 128
    scale = (128 + 64) ** -0.5
    vecs = io["vecs"]
    perm64 = cx.cf[0:64, 1, 0:64]
    with ExitStack() as esP:
        ckvn = esP.enter_context(nc.sbuf_tensor(un("ckvn"), [128, KVC, TT], BF16))
        cqn = esP.enter_context(nc.sbuf_tensor(un("cqn"), [128, QC, TOK], BF16))
        krT = esP.enter_context(nc.sbuf_tensor(un("krT"), [128, TT], BF16))
        cosQ = esP.enter_context(nc.sbuf_tensor(un("cosQ"), [128, TOK], F32))
        sinQ = esP.enter_context(nc.sbuf_tensor(un("sinQ"), [128, TOK], F32))
        emit_rope_tables(cx, io["pos_own"], TOK, vecs[:, 0:1], vecs[:, 1:2], cosQ, sinQ, "tq")
        with ExitStack() as es:
            cosK = es.enter_context(nc.sbuf_tensor(un("cosK"), [128, TT], F32))
            sinK = es.enter_context(nc.sbuf_tensor(un("sinK"), [128, TT], F32))
            emit_rope_tables(cx, io["pos_full"], TT, vecs[:, 0:1], vecs[:, 1:2], cosK, sinK, "tk")
            cx.basics(es, wcols=DC * 128, nw=3)
            rstd_ = cx.rstd
            alloc_rope_tmps(cx, es)
            hTc = es.enter_context(nc.sbuf_tensor(un("hTc"), [128, DC, 512], BF16))
            latf = es.enter_context(nc.sbuf_tensor(un("latf"), [128, max(QC, KVC), 512], F32))
            sv_TOK, sv_NT = cx.TOK, cx.NT
            cx.TOK, cx.NT = 512, 1
            for tc in range(TT // 512):
                load_hTc(cx, hTc, io["hT_full"], tc * 512)
                for blk in range(KVC):
                    wt, wn = cx.load_w(io["win_kv"][blk], DC * 128)
                    pz = 4 + cx.rot("pb2", 2)
                    emit_proj(cx, cx.psum[pz][:], "ps%d" % pz, DC, lambda kc, wt=wt, wn=wn: (wt[:, kc * 128:(kc + 1) * 128], wn),
                              lambda kc: (hTc[:, kc, :], "hTc"))
                    emit_evac(cx, latf[:, blk, :], "latf_%d" % blk, cx.psum[pz][:], "ps%d" % pz, 0)
                wt, wn = cx.load_w(io["win_kr"], DC * 64)
                pz = 4 + cx.rot("pb2", 2)
                emit_proj(cx, cx.psum[pz][0:64, :], "ps%d" % pz, DC, lambda kc, wt=wt, wn=wn: (wt[:, kc * 64:(kc + 1) * 64], wn),
                          lambda kc: (hTc[:, kc, :], "hTc"))
                emit_rope(cx, cx.psum[pz][0:64, :], "ps%d" % pz, 64, 512, cosK[0:64, tc * 512:(tc + 1) * 512], sinK[0:64, tc * 512:(tc + 1) * 512],
                          ("tk_cos", "tk_sin"), perm64, krT[0:64, tc * 512:(tc + 1) * 512], "krT")
                emit_stats(cx, None, KVC, 1.0 / KVL, None, src_sb=lambda kc: (latf[:, kc, :], "latf_%d" % kc))
                for blk in range(KVC):
                    P.emit("dve", lambda e, blk=blk, tc=tc: e.scalar_tensor_tensor(
                        out=ckvn[:, blk, tc * 512:(tc + 1) * 512], in0=latf[:, blk, :], scalar=vecs[:, 4 + QC + blk:5 + QC + blk],
                        in1=rstd_[:, 0:512], op0=ALU.mult, op1=ALU.mult), reads=["latf_%d" % blk, "rstd", "vecs"], writes=["ckvn"])
            for tq in range(TOK_full(sv_TOK) // 512):
                load_hTc(cx, hTc, io["hT_own"], tq * 512)
                for blk in range(QC):
                    wt, wn = cx.load_w(io["win_q"][blk], DC * 128)
                    pz = 4 + cx.rot("pb2", 2)
                    emit_proj(cx, cx.psum[pz][:], "ps%d" % pz, DC, lambda kc, wt=wt, wn=wn: (wt[:, kc * 128:(kc + 1) * 128], wn),
                              lambda kc: (hTc[:, kc, :], "hTc"))
                    emit_evac(cx, latf[:, blk, :], "latf_%d" % blk, cx.psum[pz][:], "ps%d" % pz, blk)
                emit_stats(cx, None, QC, 1.0 / QL, None, src_sb=lambda kc: (latf[:, kc, :], "latf_%d" % kc))
                for blk in range(QC):
                    P.emit("dve", lambda e, blk=blk, tq=tq: e.scalar_tensor_tensor(
                        out=cqn[:, blk, tq * 512:(tq + 1) * 512], in0=latf[:, blk, :], scalar=vecs[:, 4 + blk:5 + blk],
                        in1=rstd_[:, 0:512], op0=ALU.mult, op1=ALU.mult), reads=["latf_%d" % blk, "rstd", "vecs"], writes=["cqn"])
            cx.TOK, cx.NT = sv_TOK, sv_NT
            P.barrier()
        with ExitStack() as es:
            nwc = max(QC * 192, KVC * 256)
            cx.wbuf = [es.enter_context(nc.sbuf_tensor(un("w%d" % i), [128, nwc], BF16)) for i in range(4)]
            cx.nw = 4
            alloc_rope_tmps(cx, es)
            alloc_attn(cx, es, TT)
            knT = [es.enter_context(nc.sbuf_tensor(un("knT%d" % i), [128, TT], BF16)) for i in range(2)]
            Vh = [es.enter_context(nc.sbuf_tensor(un("Vh%d" % i), [128, TT], BF16)) for i in range(2)]
            qnT = [es.enter_context(nc.sbuf_tensor(un("qnT%d" % i), [128, TOK], BF16)) for i in range(2)]
            qrT = [es.enter_context(nc.sbuf_tensor(un("qrT%d" % i), [128, TOK], BF16)) for i in range(2)]
            oth = [es.enter_context(nc.sbuf_tensor(un("oth%d" % i), [128, TOK], BF16)) for i in range(2)]
            ov = io["oT"].rearrange("(h p) t -> h p t", p=128)
            ev = 0
            for h in range(MH):
                b = h % 2
                wq, wqn = cx.load_w(io["wuq"][h], QC * 192)
                wkv, wkvn = cx.load_w(io["wukv"][h], KVC * 256)
                for c in range(TT // 512):
                    pz = 4 + cx.rot("pb2", 2)
                    emit_proj(cx, cx.psum[pz][:], "ps%d" % pz, KVC, lambda kc: (wkv[:, kc * 256:kc * 256 + 128], wkvn),
                              lambda kc, c=c: (ckvn[:, kc, c * 512:(c + 1) * 512], "ckvn"))
                    emit_evac(cx, knT[b][:, c * 512:(c + 1) * 512], "knT%d" % b, cx.psum[pz][:], "ps%d" % pz, ev); ev += 1
                for t4 in range(TT // 512):
                    pz = 4 + cx.rot("pb2", 2)
                    for j in range(4):
                        t = t4 * 4 + j
                        emit_proj(cx, cx.psum[pz][:, j * 128:(j + 1) * 128], "ps%d" % pz, KVC,
                                  lambda kc, t=t: (ckvn[:, kc, t * 128:(t + 1) * 128], "ckvn"),
                                  lambda kc: (wkv[:, kc * 256 + 128:kc * 256 + 256], wkvn))
                    emit_evac(cx, Vh[b][:, t4 * 512:(t4 + 1) * 512], "Vh%d" % b, cx.psum[pz][:], "ps%d" % pz, ev); ev += 1
                for c in range(TOK // 512):
                    pz = 4 + cx.rot("pb2", 2)
                    emit_proj(cx, cx.psum[pz][:], "ps%d" % pz, QC, lambda kc: (wq[:, kc * 192:kc * 192 + 128], wqn),
                              lambda kc, c=c: (cqn[:, kc, c * 512:(c + 1) * 512], "cqn"))
                    emit_evac(cx, qnT[b][:, c * 512:(c + 1) * 512], "qnT%d" % b, cx.psum[pz][:], "ps%d" % pz, ev); ev += 1
                    pz = 4 + cx.rot("pb2", 2)
                    emit_proj(cx, cx.psum[pz][0:64, :], "ps%d" % pz, QC, lambda kc: (wq[:, kc * 192 + 128:kc * 192 + 192], wqn),
                              lambda kc, c=c: (cqn[:, kc, c * 512:(c + 1) * 512], "cqn"))
                    emit_rope(cx, cx.psum[pz][0:64, :], "ps%d" % pz, 64, 512, cosQ[0:64, c * 512:(c + 1) * 512], sinQ[0:64, c * 512:(c + 1) * 512],
                              ("tq_cos", "tq_sin"), perm64, qrT[b][0:64, c * 512:(c + 1) * 512], "qrT%d" % b)
                for i in range(NQT):
                    m = cx.rot("mk", 2)
                    P.dma("sp", cx.mk[m][:], io["maskb"][i], "mk%d" % m, writes=["mk%d" % m])

                    def pairs(c, i=i, b=b):
                        return [(qnT[b][:, i * 128:(i + 1) * 128], "qnT%d" % b, knT[b][:, c * 512:(c + 1) * 512], "knT%d" % b),
                                (qrT[b][0:64, i * 128:(i + 1) * 128], "qrT%d" % b, krT[0:64, c * 512:(c + 1) * 512], "krT")]
                    emit_attn_tile(cx, TT, pairs, lambda kb, b=b: (Vh[b][:, kb * 128:(kb + 1) * 128], "Vh%d" % b), cx.mk[m], "mk%d" % m,
                                   scale, oth[b][:, i * 128:(i + 1) * 128], "oth%d" % b)
                P.dma("sp", ov[h], oth[b][:], "oth%d" % b, reads=["oth%d" % b], writes=["oT_%d" % h])
            P.barrier()


def TOK_full(x):
    return x


ROPE_THETA = 10000.0


def tile_cols(w, col_idx, M):
    K = w.shape[0]
    KC = K // 128
    sub = w[:, col_idx]
    nb = sub.shape[1] // M
    return np.ascontiguousarray(sub.reshape(KC, 128, nb, M).transpose(2, 1, 0, 3)).reshape(nb, 128, KC * M)


def vec_cols(g):
    return np.ascontiguousarray(g.reshape(-1, 128).T)


def const_tables():
    cf = np.zeros((128, 3, 128), np.float32)
    p = np.arange(128)
    cf[p, 0, p] = 1.0
    s64 = np.where(p % 64 < 32, p + 32, p - 32)
    cf[s64, 1, p] = 1.0
    s128 = np.where(p < 64, p + 64, p - 64)
    cf[s128, 2, p] = 1.0
    return cf


def rope_vecs():
    p = np.arange(128)
    v = np.zeros((128, 4), np.float32)
    inv64 = np.power(ROPE_THETA, -np.arange(0, 64, 2, dtype=np.float32) / 64).astype(np.float32)
    inv128 = np.power(ROPE_THETA, -np.arange(0, 128, 2, dtype=np.float32) / 128).astype(np.float32)
    v[:64, 0] = inv64[p[:64] % 32]
    v[:, 1] = np.where(p % 64 < 32, -1.0, 1.0)
    v[:, 2] = inv128[p % 64]
    v[:, 3] = np.where(p < 64, -1.0, 1.0)
    return v


def causal_maskb(half, TOK, TT):
    q = half * TOK + np.arange(TOK)
    k = np.arange(TT)
    ok = (k[None, :] // 64) <= (q[:, None] // 64)
    m = np.where(ok, 0.0, NEG).astype(np.float32)
    return m.reshape(TOK // 128, 128, TT).astype(ml_dtypes.bfloat16)


def emit_dsa(cx, io, DH, DG, IH, TOPK, TT):
    P, nc, TOK, DC = cx.P, cx.nc, cx.TOK, cx.DC
    NQT = TOK // 128
    REP = DH // DG
    scale = 128 ** -0.5
    wscale = IH ** -0.5 * 128 ** -0.5
    vecs = io["vecs"]
    perm64 = cx.cf[:, 1, :]
    perm128 = cx.cf[:, 2, :]
    qscr = nc.dram_tensor(un("qscr"), [DH * 128, TOK], BF16).ap()
    qsv = qscr.rearrange("(h p) t -> h p t", p=128)
    with ExitStack() as esP:
        masks = esP.enter_context(nc.sbuf_tensor(un("masks"), [128, NQT, TT], BF16))
        with ExitStack() as esI:
            qiT = esI.enter_context(nc.sbuf_tensor(un("qiT"), [128, IH, TOK], BF16))
            kiT = esI.enter_context(nc.sbuf_tensor(un("kiT"), [128, TT], BF16))
            wiT = esI.enter_context(nc.sbuf_tensor(un("wiT"), [128, NQT, IH], F32))
            with ExitStack() as es:
                cosK = es.enter_context(nc.sbuf_tensor(un("cosK"), [128, TT], F32))
                sinK = es.enter_context(nc.sbuf_tensor(un("sinK"), [128, TT], F32))
                cosQ = es.enter_context(nc.sbuf_tensor(un("cosQ"), [128, TOK], F32))
                sinQ = es.enter_context(nc.sbuf_tensor(un("sinQ"), [128, TOK], F32))
                emit_rope_tables(cx, io["pos_full"], TT, vecs[:, 0:1], vecs[:, 1:2], cosK, sinK, "tk")
                emit_rope_tables(cx, io["pos_own"], TOK, vecs[:, 0:1], vecs[:, 1:2], cosQ, sinQ, "tq")
                cx.wbuf = [es.enter_context(nc.sbuf_tensor(un("w%d" % i), [128, DC * 128], BF16)) for i in range(3)]
                cx.nw = 3
                alloc_rope_tmps(cx, es)
                hTc = es.enter_context(nc.sbuf_tensor(un("hTc"), [128, DC, 512], BF16))
                for tc in range(TT // 512):
                    load_hTc(cx, hTc, io["hT_full"], tc * 512)
                    wt, wn = cx.load_w(io["wki"], DC * 128)
                    pz = 4 + cx.rot("pb2", 2)
                    emit_proj(cx, cx.psum[pz][:], "ps%d" % pz, DC, lambda kc, wt=wt, wn=wn: (wt[:, kc * 128:(kc + 1) * 128], wn),
                              lambda kc: (hTc[:, kc, :], "hTc"))
                    emit_rope(cx, cx.psum[pz][:], "ps%d" % pz, 128, 512, cosK[:, tc * 512:(tc + 1) * 512], sinK[:, tc * 512:(tc + 1) * 512],
                              ("tk_cos", "tk_sin"), perm64, kiT[:, tc * 512:(tc + 1) * 512], "kiT")
                for tq in range(TOK // 512):
                    load_hTc(cx, hTc, io["hT_own"], tq * 512)
                    for h in range(IH):
                        wt, wn = cx.load_w(io["wqi"][h], DC * 128)
                        pz = 4 + cx.rot("pb2", 2)
                        emit_proj(cx, cx.psum[pz][:], "ps%d" % pz, DC, lambda kc, wt=wt, wn=wn: (wt[:, kc * 128:(kc + 1) * 128], wn),
                                  lambda kc: (hTc[:, kc, :], "hTc"))
                        emit_rope(cx, cx.psum[pz][:], "ps%d" % pz, 128, 512, cosQ[:, tq * 512:(tq + 1) * 512], sinQ[:, tq * 512:(tq + 1) * 512],
                                  ("tq_cos", "tq_sin"), perm64, qiT[:, h, tq * 512:(tq + 1) * 512], "qiT_%d" % h)
                    wt, wn = cx.load_w(io["wwi"], DC * IH)
                    for j in range(4):
                        t = tq * 4 + j
                        pz = 4 + cx.rot("pb2", 2)
                        emit_proj(cx, cx.psum[pz][:, 0:IH], "ps%d" % pz, DC, lambda kc, j=j: (hTc[:, kc, j * 128:(j + 1) * 128], "hTc"),
                                  lambda kc, wt=wt, wn=wn: (wt[:, kc * IH:(kc + 1) * IH], wn))
                        P.emit("act", lambda e, t=t, pz=pz: e.activation(out=wiT[:, t, :], in_=cx.psum[pz][:, 0:IH], func=AF.Copy, scale=wscale),
                               reads=["ps%d" % pz], writes=["wiT"])
                P.barrier()
            with ExitStack() as es:
                acc = es.enter_context(nc.sbuf_tensor(un("acc"), [128, TT], F32))
                work = es.enter_context(nc.sbuf_tensor(un("work"), [128, TT], F32))
                rl = [es.enter_context(nc.sbuf_tensor(un("rl%d" % i), [128, 512], BF16)) for i in range(2)]
                mk = [es.enter_context(nc.sbuf_tensor(un("mk%d" % i), [128, TT], BF16)) for i in range(2)]
                m8 = es.enter_context(nc.sbuf_tensor(un("m8"), [128, 8], F32))
                tau = es.enter_context(nc.sbuf_tensor(un("tau"), [128, 1], F32))
                for i in range(NQT):
                    m = cx.rot("mk", 2)
                    P.dma("sp", mk[m][:], io["maskb"][i], "mk%d" % m, writes=["mk%d" % m])
                    P.emit("dve", lambda e, m=m: e.tensor_copy(out=acc[:], in_=mk[m][:]), reads=["mk%d" % m], writes=["acc"])
                    for h in range(IH):
                        for c in range(TT // 512):
                            pz = 4 + cx.rot("pb2", 2)
                            P.emit("pe", lambda e, pz=pz, h=h, i=i, c=c: e.matmul(cx.psum[pz][:], qiT[:, h, i * 128:(i + 1) * 128],
                                                                             kiT[:, c * 512:(c + 1) * 512], start=True, stop=True),
                                   reads=["qiT_%d" % h, "kiT"], writes=["ps%d" % pz])
                            r = cx.rot("rl", 2)
                            P.emit("act", lambda e, r=r, pz=pz: e.activation(out=rl[r][:], in_=cx.psum[pz][:], func=AF.Relu),
                                   reads=["ps%d" % pz], writes=["rl%d" % r])
                            P.emit("dve", lambda e, r=r, h=h, i=i, c=c: e.scalar_tensor_tensor(
                                out=acc[:, c * 512:(c + 1) * 512], in0=rl[r][:], scalar=wiT[:, i, h:h + 1], in1=acc[:, c * 512:(c + 1) * 512],
                                op0=ALU.mult, op1=ALU.add), reads=["rl%d" % r, "wiT", "acc"], writes=["acc"])
                    P.emit("act", lambda e: e.activation(out=work[:], in_=acc[:], func=AF.Copy), reads=["acc"], writes=["work"])
                    for rnd in range(TOPK // 8):
                        P.emit("dve", lambda e: e.max(out=m8[:], in_=work[:]), reads=["work"], writes=["m8"])
                        if rnd < TOPK // 8 - 1:
                            P.emit("dve", lambda e: e.match_replace(out=work[:], in_to_replace=m8[:], in_values=work[:], imm_value=3.0 * NEG),
                                   reads=["work", "m8"], writes=["work"])
                    P.emit("dve", lambda e: e.tensor_scalar(out=tau[:], in0=m8[:, 7:8], scalar1=0.5 * NEG, scalar2=None, op0=ALU.max),
                           reads=["m8"], writes=["tau"])
                    P.emit("dve", lambda e: e.tensor_scalar(out=work[:], in0=acc[:], scalar1=tau[:, 0:1], scalar2=-NEG, op0=ALU.is_ge, op1=ALU.mult),
                           reads=["acc", "tau", "work"], writes=["work"])
                    P.emit("dve", lambda e, i=i: e.tensor_scalar(out=masks[:, i, :], in0=work[:], scalar1=NEG, scalar2=None, op0=ALU.add),
                           reads=["work"], writes=["masks_%d" % i])
                P.barrier()
        KT = esP.enter_context(nc.sbuf_tensor(un("KT"), [128, DG, TT], BF16))
        V = esP.enter_context(nc.sbuf_tensor(un("V"), [128, TT // 128, DG * 128], BF16))
        with ExitStack() as es:
            cosK = es.enter_context(nc.sbuf_tensor(un("cosK"), [128, TT], F32))
            sinK = es.enter_context(nc.sbuf_tensor(un("sinK"), [128, TT], F32))
            cosQ = es.enter_context(nc.sbuf_tensor(un("cosQ"), [128, TOK], F32))
            sinQ = es.enter_context(nc.sbuf_tensor(un("sinQ"), [128, TOK], F32))
            emit_rope_tables(cx, io["pos_full"], TT, vecs[:, 2:3], vecs[:, 3:4], cosK, sinK, "tk")
            emit_rope_tables(cx, io["pos_own"], TOK, vecs[:, 2:3], vecs[:, 3:4], cosQ, sinQ, "tq")
            cx.wbuf = [es.enter_context(nc.sbuf_tensor(un("w%d" % i), [128, DC * 128], BF16)) for i in range(3)]
            cx.nw = 3
            alloc_rope_tmps(cx, es)
            hTc = es.enter_context(nc.sbuf_tensor(un("hTc"), [128, DC, 512], BF16))
            qst = [es.enter_context(nc.sbuf_tensor(un("qst%d" % i), [128, 512], BF16)) for i in range(2)]
            for tc in range(TT // 512):
                load_hTc(cx, hTc, io["hT_full"], tc * 512)
                for g in range(DG):
                    wt, wn = cx.load_w(io["wk"][g], DC * 128)
                    pz = 4 + cx.rot("pb2", 2)
                    emit_proj(cx, cx.psum[pz][:], "ps%d" % pz, DC, lambda kc, wt=wt, wn=wn: (wt[:, kc * 128:(kc + 1) * 128], wn),
                              lambda kc: (hTc[:, kc, :], "hTc"))
                    emit_rope(cx, cx.psum[pz][:], "ps%d" % pz, 128, 512, cosK[:, tc * 512:(tc + 1) * 512], sinK[:, tc * 512:(tc + 1) * 512],
                              ("tk_cos", "tk_sin"), perm128, KT[:, g, tc * 512:(tc + 1) * 512], "KT_%d" % g)
            for tq in range(TOK // 512):
                load_hTc(cx, hTc, io["hT_own"], tq * 512)
                for h in range(DH):
                    wt, wn = cx.load_w(io["wq"][h], DC * 128)
                    pz = 4 + cx.rot("pb2", 2)
                    emit_proj(cx, cx.psum[pz][:], "ps%d" % pz, DC, lambda kc, wt=wt, wn=wn: (wt[:, kc * 128:(kc + 1) * 128], wn),
                              lambda kc: (hTc[:, kc, :], "hTc"))
                    s = cx.rot("qst", 2)
                    emit_rope(cx, cx.psum[pz][:], "ps%d" % pz, 128, 512, cosQ[:, tq * 512:(tq + 1) * 512], sinQ[:, tq * 512:(tq + 1) * 512],
                              ("tq_cos", "tq_sin"), perm128, qst[s][:], "qst%d" % s)
                    P.dma("sp", qsv[h][:, tq * 512:(tq + 1) * 512], qst[s][:], "qst%d" % s, reads=["qst%d" % s], writes=["qscr_%d" % h])
            P.barrier()
        with ExitStack() as es:
            cx.wbuf = [es.enter_context(nc.sbuf_tensor(un("w%d" % i), [128, DC * 256], BF16)) for i in range(2)]
            cx.nw = 2
            hTc = es.enter_context(nc.sbuf_tensor(un("hTc"), [128, DC, 512], BF16))
            NV4 = (DG * 128) // 256
            ev = 0
            for tc in range(TT // 512):
                load_hTc(cx, hTc, io["hT_full"], tc * 512)
                for qv in range(NV4):
                    wt, wn = cx.load_w(io["wv"][qv], DC * 256)
                    for jj in range(2):
                        pz = 4 + cx.rot("pb2", 2)
                        for j2 in range(2):
                            j = jj * 2 + j2
                            emit_proj(cx, cx.psum[pz][:, j2 * 256:(j2 + 1) * 256], "ps%d" % pz, DC,
                                      lambda kc, j=j: (hTc[:, kc, j * 128:(j + 1) * 128], "hTc"),
                                      lambda kc, wt=wt, wn=wn: (wt[:, kc * 256:(kc + 1) * 256], wn))
                            emit_evac(cx, V[:, tc * 4 + j, qv * 256:(qv + 1) * 256], "V", cx.psum[pz][:, j2 * 256:(j2 + 1) * 256], "ps%d" % pz, ev)
                            ev += 1
            P.barrier()
        with ExitStack() as es:
            alloc_attn(cx, es, TT)
            qh = [es.enter_context(nc.sbuf_tensor(un("qh%d" % i), [128, TOK], BF16)) for i in range(2)]
            oth = [es.enter_context(nc.sbuf_tensor(un("oth%d" % i), [128, TOK], BF16)) for i in range(2)]
            ov = io["oT"].rearrange("(h p) t -> h p t", p=128)
            for h in range(DH):
                g = h // REP
                b = h % 2
                P.dma("sp", qh[b][:], qsv[h], "qh%d" % b, reads=["qscr_%d" % h], writes=["qh%d" % b])
                for i in range(NQT):
                    def pairs(c, i=i, b=b, g=g):
                        return [(qh[b][:, i * 128:(i + 1) * 128], "qh%d" % b, KT[:, g, c * 512:(c + 1) * 512], "KT_%d" % g)]
                    emit_attn_tile(cx, TT, pairs, lambda kb, g=g: (V[:, kb, g * 128:(g + 1) * 128], "V"), masks[:, i, :], "masks_%d" % i,
                                   scale, oth[b][:, i * 128:(i + 1) * 128], "oth%d" % b)
                P.dma("sp", ov[h], oth[b][:], "oth%d" % b, reads=["oth%d" % b], writes=["oT_%d" % h])
            P.barrier()


GN_EPS = 64e-5


def emit_rwkv(cx, io, RC, TT, GL, scan_split=True):
    P, nc, DC = cx.P, cx.nc, cx.DC
    RCB = RC // 128
    G = RCB
    NCH = TT // 512
    GLC = [(i * 128, min(128, GL - i * 128)) for i in range((GL + 127) // 128)]
    vecs = io["vecs"]
    vo = 6 * DC

    def vcol(k, cb):
        return vecs[:, vo + k * RCB + cb: vo + k * RCB + cb + 1]
    def scr(name, shape, dt=F32):
        return nc.dram_tensor(un(name), list(shape), dt).ap()
    rT, kT, vT, kpT, yT = (scr(n, [RC, TT]) for n in ("rT", "kT", "vT", "kpT", "yT"))
    gT = scr("gT", [RC, TT], BF16)
    tokm = scr("tokm", [TT, 5, 2, G * 64])
    bones = cx.cf[:, 1, :]
    with ExitStack() as esP:
        bo = esP.enter_context(nc.sbuf_tensor(un("bones"), [128, 128], F32))
        P.emit("dve", lambda e: e.memset(bo[:], 0.0), writes=["bones"])
        P.emit("dve", lambda e: e.memset(bo[0:64, 0:64], 1.0), reads=["bones"], writes=["bones"])
        P.emit("dve", lambda e: e.memset(bo[64:128, 64:128], 1.0), reads=["bones"], writes=["bones"])
        with ExitStack() as es:
            cx.wbuf = [es.enter_context(nc.sbuf_tensor(un("w%d" % i), [128, DC * 128], BF16)) for i in range(3)]
            cx.nw = 3
            hx = es.enter_context(nc.sbuf_tensor(un("hx"), [128, DC, 514], BF16))
            xx = es.enter_context(nc.sbuf_tensor(un("xx"), [128, DC, 512], BF16))
            xm = [es.enter_context(nc.sbuf_tensor(un("xm%d" % i), [128, DC, 512], BF16)) for i in range(1)]
            tw = es.enter_context(nc.sbuf_tensor(un("tw"), [128, 512], BF16))
            ta = es.enter_context(nc.sbuf_tensor(un("ta"), [128, 512], BF16))
            tg = es.enter_context(nc.sbuf_tensor(un("tg"), [128, len(GLC), 512], BF16))
            st3 = [es.enter_context(nc.sbuf_tensor(un("st3_%d" % i), [128, 512], F32)) for i in range(3)]
            NE = 12
            et = [[es.enter_context(nc.sbuf_tensor(un("et%d_%d" % (j, i)), [128, 512], F32)) for i in range(NE)] for j in range(2)]
            gst = [es.enter_context(nc.sbuf_tensor(un("gst%d" % i), [128, 512], BF16)) for i in range(2)]
            tst = [es.enter_context(nc.sbuf_tensor(un("tst%d" % i), [128, 512], F32)) for i in range(3)]
            hv = io["hT_full"].rearrange("(c p) t -> p c t", p=128)
            P.emit("dve", lambda e: e.memset(hx[:, :, 0:2], 0.0), writes=["hx"])
            evc = 0
            for tc in range(NCH):
                c0 = tc * 512
                if tc > 0:
                    P.emit("dve", lambda e: e.tensor_copy(out=hx[:, :, 1:2], in_=hx[:, :, 513:514]), reads=["hx"], writes=["hx"])
                P.dma("sp", hx[:, :, 2:514], hv[:, :, c0:c0 + 512], "hTc", reads=["hx"], writes=["hx"])
                for kc in range(DC):
                    P.emit("dve", lambda e, kc=kc: e.tensor_tensor(out=xx[:, kc, :], in0=hx[:, kc, 1:513], in1=hx[:, kc, 2:514], op=ALU.subtract),
                           reads=["hx"], writes=["xx_%d" % kc])

                def mix(mi):
                    s = cx.rot("xm", 1)
                    for kc in range(DC):
                        P.emit("dve", lambda e, kc=kc, s=s, mi=mi: e.scalar_tensor_tensor(
                            out=xm[s][:, kc, :], in0=xx[:, kc, :], scalar=vecs[:, mi * DC + kc: mi * DC + kc + 1], in1=hx[:, kc, 2:514],
                            op0=ALU.mult, op1=ALU.add), reads=["xx_%d" % kc, "hx", "vecs"], writes=["xm%d_%d" % (s, kc)])
                    return s

                def projfm(s, wslab, M, psap, psn):
                    wt, wn = cx.load_w(wslab, DC * M)
                    emit_proj(cx, psap, psn, DC, lambda kc: (wt[:, kc * M:(kc + 1) * M], wn), lambda kc: (xm[s][:, kc, :], "xm%d_%d" % (s, kc)))
                s = mix(1)
                pz = 4 + cx.rot("pb2", 2)
                projfm(s, io["w1"], 128, cx.psum[pz][:], "ps%d" % pz)
                P.emit("act", lambda e, pz=pz: e.activation(out=tw[:], in_=cx.psum[pz][:], func=AF.Tanh), reads=["ps%d" % pz], writes=["tw"])
                s = mix(4)
                pz = 4 + cx.rot("pb2", 2)
                projfm(s, io["a1"], 128, cx.psum[pz][:], "ps%d" % pz)
                P.emit("act", lambda e, pz=pz: e.activation(out=ta[:], in_=cx.psum[pz][:], func=AF.Copy), reads=["ps%d" % pz], writes=["ta"])
                s = mix(5)
                for gi, (g0, gm) in enumerate(GLC):
                    pz = 4 + cx.rot("pb2", 2)
                    projfm(s, io["g1"][gi][:, 0:DC * gm] if gm == 128 else io["g1b"], gm, cx.psum[pz][0:gm, :], "ps%d" % pz)
                    P.emit("act", lambda e, pz=pz, gi=gi, gm=gm: e.activation(out=tg[0:gm, gi, :], in_=cx.psum[pz][0:gm, :], func=AF.Sigmoid),
                           reads=["ps%d" % pz], writes=["tg"])
                for mi, wkey, dst in ((0, "wr", rT), (2, "wk", kT), (3, "wv", vT)):
                    s = mix(mi)
                    dv = dview(dst)
                    for cb in range(RCB):
                        pz = 4 + cx.rot("pb2", 2)
                        projfm(s, io[wkey][cb], 128, cx.psum[pz][:], "ps%d" % pz)
                        q = cx.rot("st3", 3)
                        emit_evac(cx, st3[q][:], "st3_%d" % q, cx.psum[pz][:], "ps%d" % pz, evc); evc += 1
                        P.dma("sp", dv[cb][:, c0:c0 + 512], st3[q][:], "st3_%d" % q, reads=["st3_%d" % q], writes=["%s_%d" % (wkey, cb)])
                for cb in range(RCB):
                    j = cx.rot("et", 2)
                    E = et[j]
                    en = ["et%d_%d" % (j, i) for i in range(NE)]
                    for i, (src, key) in enumerate(((rT, "wr"), (kT, "wk"), (vT, "wv"))):
                        P.dma("sp", E[i][:], dview(src)[cb][:, c0:c0 + 512], "etl%d_%d" % (j, i), reads=["%s_%d" % (key, cb)], writes=[en[i]])
                    r_, k_, v_ = E[0], E[1], E[2]
                    w2t, w2n = cx.load_w(io["w2"][cb], 128)
                    pz = 4 + cx.rot("pb2", 2)
                    P.emit("pe", lambda e, pz=pz, w2t=w2t: e.matmul(cx.psum[pz][:], w2t[:, 0:128], tw[:], start=True, stop=True),
                           reads=[w2n, "tw"], writes=["ps%d" % pz])
                    P.emit("act", lambda e, pz=pz, cb=cb, E=E: e.activation(out=E[3][:], in_=cx.psum[pz][:], func=AF.Sigmoid, bias=vcol(0, cb)),
                           reads=["ps%d" % pz, "vecs"], writes=[en[3]])
                    P.emit("act", lambda e, E=E: e.activation(out=E[4][:], in_=E[3][:], func=AF.Exp, scale=-math.exp(-0.5)),
                           reads=[en[3]], writes=[en[4]])
                    a2t, a2n = cx.load_w(io["a2"][cb], 128)
                    pz = 4 + cx.rot("pb2", 2)
                    P.emit("pe", lambda e, pz=pz, a2t=a2t: e.matmul(cx.psum[pz][:], a2t[:, 0:128], ta[:], start=True, stop=True),
                           reads=[a2n, "ta"], writes=["ps%d" % pz])
                    P.emit("act", lambda e, pz=pz, cb=cb, E=E: e.activation(out=E[5][:], in_=cx.psum[pz][:], func=AF.Sigmoid, bias=vcol(1, cb)),
                           reads=["ps%d" % pz, "vecs"], writes=[en[5]])
                    g2t, g2n = cx.load_w(io["g2"][cb], len(GLC) * 128)
                    pz = 4 + cx.rot("pb2", 2)
                    for gi, (g0, gm) in enumerate(GLC):
                        P.emit("pe", lambda e, pz=pz, g2t=g2t, gi=gi, gm=gm: e.matmul(cx.psum[pz][:], g2t[0:gm, gi * 128:(gi + 1) * 128], tg[0:gm, gi, :],
                                                                                start=(gi == 0), stop=(gi == len(GLC) - 1)),
                               reads=[g2n, "tg"], writes=["ps%d" % pz])
                    gq = cx.rot("gst", 2)
                    P.emit("act", lambda e, pz=pz, gq=gq: e.activation(out=gst[gq][:], in_=cx.psum[pz][:], func=AF.Copy), reads=["ps%d" % pz], writes=["gst%d" % gq])
                    P.dma("sp", dview(gT)[cb][:, c0:c0 + 512], gst[gq][:], "gst%d" % gq, reads=["gst%d" % gq], writes=["gT_%d" % cb])
                    P.emit("dve", lambda e, E=E, cb=cb: e.tensor_scalar(out=E[6][:], in0=E[1][:], scalar1=vcol(2, cb), scalar2=None, op0=ALU.mult),
                           reads=[en[1], "vecs"], writes=[en[6]])
                    P.emit("act", lambda e, E=E: e.activation(out=E[7][:], in_=E[6][:], func=AF.Square), reads=[en[6]], writes=[en[7]])
                    pz = 4 + cx.rot("pb2", 2)
                    P.emit("pe", lambda e, pz=pz, E=E: e.matmul(cx.psum[pz][:], bo[:], E[7][:], start=True, stop=True),
                           reads=["bones", en[7]], writes=["ps%d" % pz])
                    P.emit("act", lambda e, pz=pz, E=E: e.activation(out=E[7][:], in_=cx.psum[pz][:], func=AF.Sqrt), reads=["ps%d" % pz], writes=[en[7]])
                    P.emit("dve", lambda e, E=E: e.tensor_scalar(out=E[7][:], in0=E[7][:], scalar1=1e-12, scalar2=None, op0=ALU.max), reads=[en[7]], writes=[en[7]])
                    P.emit("dve", lambda e, E=E: e.reciprocal(out=E[7][:], in_=E[7][:]), reads=[en[7]], writes=[en[7]])
                    P.emit("dve", lambda e, E=E: e.scalar_tensor_tensor(out=E[8][:], in0=E[6][:], scalar=-1.0, in1=E[7][:], op0=ALU.mult, op1=ALU.mult),
                           reads=[en[6], en[7]], writes=[en[8]])
                    P.emit("dve", lambda e, E=E: e.scalar_tensor_tensor(out=E[9][:], in0=E[8][:], scalar=-1.0, in1=E[5][:], op0=ALU.mult, op1=ALU.mult),
                           reads=[en[8], en[5]], writes=[en[9]])
                    P.emit("dve", lambda e, E=E, cb=cb: e.tensor_scalar(out=E[10][:], in0=E[5][:], scalar1=-1.0, scalar2=vcol(3, cb), op0=ALU.add, op1=ALU.mult),
                           reads=[en[5], "vecs"], writes=[en[10]])
                    P.emit("dve", lambda e, E=E: e.scalar_tensor_tensor(out=E[10][:], in0=E[10][:], scalar=1.0, in1=E[1][:], op0=ALU.add, op1=ALU.mult),
                           reads=[en[10], en[1]], writes=[en[10]])
                    P.dma("sp", dview(kpT)[cb][:, c0:c0 + 512], E[10][:], "kps%d" % j, reads=[en[10]], writes=["kpT_%d" % cb])
                    for oi, src_i in enumerate((4, 8, 9, 10, 0)):
                        pz = 4 + cx.rot("pb2", 2)
                        for jt in range(4):
                            P.emit("pe", lambda e, pz=pz, jt=jt, E=E, src_i=src_i: e.transpose(
                                out=cx.psum[pz][:, jt * 128:(jt + 1) * 128], in_=E[src_i][:, jt * 128:(jt + 1) * 128], identity=cx.cf[:, 0, :]),
                                reads=[en[src_i], "cf"], writes=["ps%d" % pz])
                        tq = cx.rot("tst", 3)
                        emit_evac(cx, tst[tq][:], "tst%d" % tq, cx.psum[pz][:], "ps%d" % pz, evc); evc += 1
                        for jt in range(4):
                            dstv = tokm[c0 + jt * 128:c0 + (jt + 1) * 128, oi, :, cb * 64:(cb + 1) * 64]
                            P.dma("sp", dstv, tst[tq][:, jt * 128:(jt + 1) * 128].rearrange("p (h k) -> p h k", h=2), "tst%d" % tq,
                                  reads=["tst%d" % tq], writes=["tokm_%d_%d_%d_%d" % (tc, cb, oi, jt)], grp=("tk", tc, cb, oi))
            P.barrier()
        with ExitStack() as es:
            TC = 2
            bc = [es.enter_context(nc.sbuf_tensor(un("bc%d" % i), [128, TC, 5, G * 64], F32)) for i in range(2)]
            S = [es.enter_context(nc.sbuf_tensor(un("S%d" % i), [128, G, 64], F32)) for i in range(2)]
            tmp = es.enter_context(nc.sbuf_tensor(un("tmp"), [128, G, 64], F32))
            tmp2 = es.enter_context(nc.sbuf_tensor(un("tmp2"), [128, G, 64], F32))
            tmp3 = es.enter_context(nc.sbuf_tensor(un("tmp3"), [128, G, 64], F32))
            sa = es.enter_context(nc.sbuf_tensor(un("sa"), [128, G], F32))
            vch = [es.enter_context(nc.sbuf_tensor(un("vch%d" % i), [128, G, 512], F32)) for i in range(1)]
            ych = [es.enter_context(nc.sbuf_tensor(un("ych%d" % i), [128, G, 512], F32)) for i in range(1)]
            P.emit("dve", lambda e: e.memset(S[0][:], 0.0), writes=["S0"])
            e2 = "pool" if scan_split else "dve"
            vv = vT.rearrange("(g p) t -> p g t", p=128)
            yv = yT.rearrange("(g p) t -> p g t", p=128)
            for tb in range(TT // 512):
                P.dma("sp", vch[0][:], vv[:, :, tb * 512:(tb + 1) * 512], "vch", reads=["wv_%d" % i for i in range(RCB)], writes=["vch"])
                for t2 in range(512 // TC):
                    t0 = tb * 512 + t2 * TC
                    bq = cx.rot("bc", 2)
                    for hp in range(2):
                        P.dma("sp", bc[bq][hp * 64:(hp + 1) * 64], tokm[t0:t0 + TC, :, hp, :].partition_broadcast(64), "bc%d_%d" % (bq, hp),
                              writes=["bc%d_%d" % (bq, hp)])
                    for j in range(TC):
                        tl = t2 * TC + j
                        t = t0 + j
                        s0, s1 = S[t % 2], S[(t + 1) % 2]
                        n0, n1 = "S%d" % (t % 2), "S%d" % ((t + 1) % 2)
                        Wv, Av, Bv, Kv, Rv = (bc[bq][:, j, oi, :].rearrange("p (g k) -> p g k", k=64) for oi in range(5))
                        bn = "bc%d_0" % bq
                        bn1 = "bc%d_1" % bq
                        P.emit(e2, lambda e, Kv=Kv, tl=tl: e.tensor_tensor(out=tmp2[:], in0=Kv, in1=vch[0][:, :, tl:tl + 1].to_broadcast([128, G, 64]), op=ALU.mult),
                               reads=[bn, bn1, "vch"], writes=["tmp2"])
                        P.emit("dve", lambda e, s0=s0, Av=Av: e.tensor_tensor(out=tmp[:], in0=s0[:], in1=Av, op=ALU.mult), reads=[n0, bn, bn1], writes=["tmp"])
                        P.emit("dve", lambda e: e.tensor_reduce(out=sa[:], in_=tmp[:], axis=AX.X, op=ALU.add), reads=["tmp"], writes=["sa"])
                        P.emit("dve", lambda e, s0=s0, s1=s1, Wv=Wv: e.tensor_tensor(out=s1[:], in0=s0[:], in1=Wv, op=ALU.mult), reads=[n0, bn, bn1], writes=[n1])
                        P.emit("dve", lambda e, Bv=Bv: e.tensor_tensor(out=tmp[:], in0=Bv, in1=sa[:].unsqueeze(2).to_broadcast([128, G, 64]), op=ALU.mult),
                               reads=[bn, bn1, "sa", "tmp"], writes=["tmp"])
                        P.emit("dve", lambda e, s1=s1: e.tensor_tensor(out=s1[:], in0=s1[:], in1=tmp[:], op=ALU.add), reads=[n1, "tmp"], writes=[n1])
                        P.emit("dve", lambda e, s1=s1: e.tensor_tensor(out=s1[:], in0=s1[:], in1=tmp2[:], op=ALU.add), reads=[n1, "tmp2"], writes=[n1])
                        P.emit(e2, lambda e, s1=s1, Rv=Rv: e.tensor_tensor(out=tmp3[:], in0=s1[:], in1=Rv, op=ALU.mult), reads=[n1, bn, bn1], writes=["tmp3"])
                        P.emit(e2, lambda e, tl=tl: e.tensor_reduce(out=ych[0][:, :, tl], in_=tmp3[:], axis=AX.X, op=ALU.add), reads=["tmp3"], writes=["ych"])
                P.dma("sp", yv[:, :, tb * 512:(tb + 1) * 512], ych[0][:], "ycho", reads=["ych"], writes=["yT"])
            P.barrier()
        with ExitStack() as es:
            NE = 8
            et = [[es.enter_context(nc.sbuf_tensor(un("dt%d_%d" % (j, i)), [128, 512], F32)) for i in range(NE)] for j in range(2)]
            gin = [es.enter_context(nc.sbuf_tensor(un("gin%d" % i), [128, 512], BF16)) for i in range(2)]
            yo = [es.enter_context(nc.sbuf_tensor(un("yo%d" % i), [128, 512], BF16)) for i in range(2)]
            gneps = es.enter_context(nc.sbuf_tensor(un("gneps"), [128, 1], F32))
            P.emit("dve", lambda e: e.memset(gneps[:], GN_EPS), writes=["gneps"])
            ov = dview(io["ygT"])
            for tc in range(NCH):
                c0 = tc * 512
                for cb in range(RCB):
                    j = cx.rot("dt", 2)
                    E = et[j]
                    en = ["dt%d_%d" % (j, i) for i in range(NE)]
                    for i, src in enumerate((yT, rT, kpT, vT)):
                        P.dma("sp", E[i][:], dview(src)[cb][:, c0:c0 + 512], "dtl%d_%d" % (j, i), writes=[en[i]])
                    P.dma("sp", gin[j][:], dview(gT)[cb][:, c0:c0 + 512], "gin%d" % j, writes=["gin%d" % j])
                    pz = 4 + cx.rot("pb2", 2)
                    P.emit("pe", lambda e, pz=pz, E=E: e.matmul(cx.psum[pz][:], bo[:], E[0][:], start=True, stop=True), reads=["bones", en[0]], writes=["ps%d" % pz])
                    P.emit("dve", lambda e, pz=pz, E=E: e.scalar_tensor_tensor(out=E[4][:], in0=cx.psum[pz][:], scalar=-1.0 / 64, in1=E[0][:], op0=ALU.mult, op1=ALU.add),
                           reads=["ps%d" % pz, en[0]], writes=[en[4]])
                    P.emit("act", lambda e, E=E: e.activation(out=E[5][:], in_=E[4][:], func=AF.Square), reads=[en[4]], writes=[en[5]])
                    pz = 4 + cx.rot("pb2", 2)
                    P.emit("pe", lambda e, pz=pz, E=E: e.matmul(cx.psum[pz][:], bo[:], E[5][:], start=True, stop=True), reads=["bones", en[5]], writes=["ps%d" % pz])
                    P.emit("act", lambda e, pz=pz, E=E: e.activation(out=E[5][:], in_=cx.psum[pz][:], func=AF.Sqrt, scale=1.0 / 64, bias=gneps[:, 0:1]),
                           reads=["ps%d" % pz, "gneps"], writes=[en[5]])
                    P.emit("dve", lambda e, E=E: e.reciprocal(out=E[5][:], in_=E[5][:]), reads=[en[5]], writes=[en[5]])
                    P.emit("dve", lambda e, E=E: e.tensor_tensor(out=E[4][:], in0=E[4][:], in1=E[5][:], op=ALU.mult), reads=[en[4], en[5]], writes=[en[4]])
                    P.emit("dve", lambda e, E=E, cb=cb: e.tensor_scalar(out=E[4][:], in0=E[4][:], scalar1=vcol(4, cb), scalar2=vcol(5, cb), op0=ALU.mult, op1=ALU.add),
                           reads=[en[4], "vecs"], writes=[en[4]])
                    P.emit("dve", lambda e, E=E, cb=cb: e.scalar_tensor_tensor(out=E[6][:], in0=E[1][:], scalar=vcol(6, cb), in1=E[2][:], op0=ALU.mult, op1=ALU.mult),
                           reads=[en[1], en[2], "vecs"], writes=[en[6]])
                    pz = 4 + cx.rot("pb2", 2)
                    P.emit("pe", lambda e, pz=pz, E=E: e.matmul(cx.psum[pz][:], bo[:], E[6][:], start=True, stop=True), reads=["bones", en[6]], writes=["ps%d" % pz])
                    P.emit("dve", lambda e, pz=pz, E=E: e.tensor_tensor(out=E[6][:], in0=cx.psum[pz][:], in1=E[3][:], op=ALU.mult), reads=["ps%d" % pz, en[3]], writes=[en[6]])
                    P.emit("dve", lambda e, E=E: e.tensor_tensor(out=E[4][:], in0=E[4][:], in1=E[6][:], op=ALU.add), reads=[en[4], en[6]], writes=[en[4]])
                    P.emit("dve", lambda e, E=E, j=j: e.tensor_tensor(out=yo[j][:], in0=E[4][:], in1=gin[j][:], op=ALU.mult), reads=[en[4], "gin%d" % j], writes=["yo%d" % j])
                    P.dma("sp", ov[cb][:, c0:c0 + 512], yo[j][:], "yo%d" % j, reads=["yo%d" % j], writes=["yg_%d" % cb])
            P.barrier()


class Cfg:
    D = 4096; F = 6144; B = 4; SEQ = 2048; NSPLIT = 2
    MH = 32; QL = 1024; KVL = 512
    DH = 32; DG = 8; IH = 32; TOPK = 256
    GL = 480


CFG = Cfg()


def emit_tail(cx, cfg, x_in, xkey, o_src, KO, wo, ffns, vecs, norm_out, final, dests):
    P, nc, TOK, DC = cx.P, cx.nc, cx.TOK, cx.DC
    cur, ckey = x_in, xkey
    si = 0
    if o_src is not None:
        with ExitStack() as es:
            cx.basics(es)
            oT = es.enter_context(nc.sbuf_tensor(un("oTs"), [128, KO, TOK], BF16))
            P.dma("sp", oT[:], o_src.rearrange("(h p) t -> p h t", p=128), "oTs", writes=["oTs_%d" % k for k in range(KO)])
            dst = dests[si]; si += 1
            emit_down(cx, oT, "oTs", KO, lambda i: wo[i], cur, ckey, dst[0], dst[1], 1.0)
            cur, ckey = dst
            P.barrier()
    for (wgu, wd, gcol) in ffns:
        with ExitStack() as es:
            cx.basics(es)
            hT = es.enter_context(nc.sbuf_tensor(un("hT"), [128, DC, TOK], BF16))
            aT = es.enter_context(nc.sbuf_tensor(un("aT"), [128, cfg.F // 128 // cfg.NSPLIT, TOK], BF16))
            dst = dests[si]; si += 1
            emit_ffn(cx, cur, ckey, dst[0], dst[1], wgu, wd, vecs[:, gcol:gcol + DC], "vecs", hT, aT, cfg.F, cfg.NSPLIT)
            cur, ckey = dst
            P.barrier()
    with ExitStack() as es:
        cx.basics(es)
        ap, gcol = norm_out
        emit_norm(cx, cur, ckey, vecs[:, gcol:gcol + DC], "vecs", out_dram=ap, out_dt=(F32 if final else BF16), okey="nout")
        P.barrier()
    return cur, ckey


def build_program(cfg, kind):
    nc = bass.Bass("TRN2", target_bir_lowering=False)
    D, F, TOK, TT = cfg.D, cfg.F, cfg.SEQ // 2, cfg.SEQ
    DC, FB = D // 128, F // 128
    FH = FB // cfg.NSPLIT

    def inp(name, shape, dt=F32):
        return nc.dram_tensor(name, list(shape), dt, kind="ExternalInput").ap()

    def outp(name, shape, dt=F32):
        return nc.dram_tensor(name, list(shape), dt, kind="ExternalOutput").ap()

    consts = inp("consts", [128, 3, 128])
    cx = Ctx(nc, D, TOK, consts)
    P = cx.P

    def ffn_in(tag):
        return (inp("wgu" + tag, [FB, 2, 128, DC * 128]), inp("wd" + tag, [cfg.NSPLIT, DC, 128, FH * 128]))

    def load_vecs(nv):
        vd = inp("vecs_in", [128, nv])
        v = nc.alloc_sbuf_tensor("vecs", [128, nv], F32)
        P.dma("sp", v[:], vd, "vecs", writes=["vecs"])
        return v
    scr = [(nc.dram_tensor("xs%d" % i, [D, TOK], F32).ap(), "xs%d" % i) for i in range(2)]
    if kind == "first":
        x = inp("xT", [D, TOK])
        vecs = load_vecs(2 * DC)
        f = ffn_in("A")
        xo = outp("xo", [D, TOK])
        ho = outp("ho", [D, TOK], BF16)
        emit_tail(cx, cfg, x, "xin", None, 0, None, [(f[0], f[1], 0)], vecs, (ho, DC), False, [(xo, "xo")])
    elif kind in ("mla", "mla_last", "dsa"):
        x = inp("xT", [D, TOK])
        io = dict(hT_full=inp("hT_full", [D, TT], BF16), hT_own=inp("hT_own", [D, TOK], BF16),
                  pos_full=inp("pos_full", [1, TT], I32), pos_own=inp("pos_own", [1, TOK], I32),
                  maskb=inp("maskb", [TOK // 128, 128, TT], BF16))
        nff = 1 if kind == "mla_last" else 2
        if kind == "dsa":
            DH, DG, IH = cfg.DH, cfg.DG, cfg.IH
            io.update(wq=inp("wq", [DH, 128, DC * 128]), wk=inp("wk", [DG, 128, DC * 128]), wv=inp("wv", [DG * 128 // 256, 128, DC * 256]),
                      wqi=inp("wqi", [IH, 128, DC * 128]), wki=inp("wki", [128, DC * 128]), wwi=inp("wwi", [128, DC * IH]))
            nmv = 4
            KO = DH
        else:
            QC, KVC = cfg.QL // 128, cfg.KVL // 128
            io.update(win_q=inp("win_q", [QC, 128, DC * 128]), win_kv=inp("win_kv", [KVC, 128, DC * 128]), win_kr=inp("win_kr", [128, DC * 64]),
                      wuq=inp("wuq", [cfg.MH, 128, QC * 192]), wukv=inp("wukv", [cfg.MH, 128, KVC * 256]))
            nmv = 4 + QC + KVC
            KO = cfg.MH
        vecs = load_vecs(nmv + (nff + 1) * DC)
        io["vecs"] = vecs
        wo = inp("wo", [DC, 128, KO * 128])
        ffs = [ffn_in(t) for t in ("A", "B")[:nff]]
        oscr = nc.dram_tensor("oscr", [KO * 128, TOK], BF16).ap()
        io["oT"] = oscr
        if kind == "dsa":
            emit_dsa(cx, io, cfg.DH, cfg.DG, cfg.IH, cfg.TOPK, TT)
        else:
            emit_mla(cx, io, cfg.MH, cfg.QL, cfg.KVL, TT)
        xo = outp("xo", [D, TOK])
        if kind == "mla_last":
            emit_tail(cx, cfg, x, "xin", oscr, KO, wo, [(ffs[0][0], ffs[0][1], nmv)], vecs, (xo, nmv + DC), True, [scr[0], scr[1]])
        else:
            ho = outp("ho", [D, TOK], BF16)
            emit_tail(cx, cfg, x, "xin", oscr, KO, wo, [(ffs[0][0], ffs[0][1], nmv), (ffs[1][0], ffs[1][1], nmv + DC)], vecs,
                      (ho, nmv + 2 * DC), False, [scr[0], scr[1], (xo, "xo")])
    elif kind == "rwkv":
        RC = D // 2
        RCB = RC // 128
        GL = cfg.GL
        ng = (GL + 127) // 128
        io = dict(hT_full=inp("hT_full", [D, TT], BF16))
        for k in ("wr", "wk", "wv"):
            io[k] = inp(k, [RCB, 128, DC * 128])
        io["w1"] = inp("w1", [128, DC * 128]); io["a1"] = inp("a1", [128, DC * 128])
        io["w2"] = inp("w2", [RCB, 128, 128]); io["a2"] = inp("a2", [RCB, 128, 128])
        io["g1"] = inp("g1", [GL // 128, 128, DC * 128]); io["g1b"] = inp("g1b", [128, DC * (GL % 128)])
        io["g2"] = inp("g2", [RCB, 128, ng * 128])
        io["vecs"] = load_vecs(6 * DC + 7 * RCB)
        io["ygT"] = outp("ygT", [RC, TT], BF16)
        cxr = cx
        cxr.TOK, cxr.NT = 512, 1
        emit_rwkv(cx, io, RC, TT, GL, scan_split=False)
    elif kind == "post":
        x = inp("xT", [D, TOK])
        yg = inp("yg", [D, TOK], BF16)
        vecs = load_vecs(3 * DC)
        wo = inp("wo", [DC, 128, DC * 128])
        ffs = [ffn_in("A"), ffn_in("B")]
        xo = outp("xo", [D, TOK])
        ho = outp("ho", [D, TOK], BF16)
        emit_tail(cx, cfg, x, "xin", yg, DC, wo, [(ffs[0][0], ffs[0][1], 0), (ffs[1][0], ffs[1][1], DC)], vecs, (ho, 2 * DC), False, [scr[0], scr[1], (xo, "xo")])
    P.barrier()
    cx.stats = P.finalize()
    return nc


def ffn_tiles(cfg, w_gu, w_down):
    D, F = cfg.D, cfg.F
    DC, FB = D // 128, F // 128
    FH = FB // cfg.NSPLIT
    a = w_gu.reshape(DC, 128, 2, FB, 128)
    wgu = np.ascontiguousarray(a.transpose(3, 2, 1, 0, 4)).reshape(FB, 2, 128, DC * 128)
    b = w_down.reshape(cfg.NSPLIT, FH, 128, DC, 128)
    wd = np.ascontiguousarray(b.transpose(0, 3, 2, 1, 4)).reshape(cfg.NSPLIT, DC, 128, FH * 128)
    return wgu, wd


_PROGS = {}


def get_prog(cfg, kind):
    key = (id(cfg), kind)
    if key not in _PROGS:
        _PROGS[key] = build_program(cfg, kind)
    return _PROGS[key]


def launch(cfg, kind, shared, percore):
    import time as _t
    t0 = _t.time()
    nc = get_prog(cfg, kind)
    print("[kernel] launch %s: built %.1fs" % (kind, _t.time() - t0), flush=True)
    n = len(percore)
    in_maps = []
    for c in range(n):
        m = dict(shared)
        m.update(percore[c])
        in_maps.append(m)
    res = run_bass_kernel_spmd(nc, in_maps, core_ids=list(range(n)))
    print("[kernel] launch %s: done %.1fs" % (kind, _t.time() - t0), flush=True)
    return res.results


def run_model(cfg, inp):
    D, F, B, SEQ = cfg.D, cfg.F, cfg.B, cfg.SEQ
    TOK, TT = SEQ // 2, SEQ
    NCORE = 2 * B
    f32 = lambda a: np.ascontiguousarray(np.asarray(a, dtype=np.float32))
    x = f32(inp["x"])
    pos = np.ascontiguousarray(np.asarray(inp["positions"], dtype=np.int32))
    consts = const_tables()
    rv = rope_vecs()
    core = [(c // 2, c % 2) for c in range(NCORE)]
    masks = [causal_maskb(h, TOK, TT) for h in range(2)]

    def ffn(i, j, tag):
        wgu, wd = ffn_tiles(cfg, f32(inp["ffn_w_gu_%d" % i][j]), f32(inp["ffn_w_down_%d" % i][j]))
        return {"wgu" + tag: wgu, "wd" + tag: wd}

    def fnorm(i, j):
        return vec_cols(f32(inp["ffn_norm_%d" % i][j]))

    def mnorm(i):
        return vec_cols(f32(inp["mix_norm_%d" % i]))

    def full_h(ho):
        return [np.ascontiguousarray(np.concatenate([ho[2 * b], ho[2 * b + 1]], axis=1)) for b in range(B)]

    def attn_percore(xo, ho):
        hf = full_h(ho)
        return [dict(xT=xo[c], hT_full=hf[b], hT_own=ho[c], pos_full=pos[b][None, :], pos_own=np.ascontiguousarray(pos[b][None, h * TOK:(h + 1) * TOK]),
                     maskb=masks[h]) for c, (b, h) in enumerate(core)]

    def mla_w(p):
        w_in, w_uq, w_ukv, w_o = (f32(inp[p + k]) for k in ("w_in", "w_uq", "w_ukv", "w_o"))
        QL, KVL, MH = cfg.QL, cfg.KVL, cfg.MH
        return dict(win_q=tile_cols(w_in, np.arange(QL), 128), win_kv=tile_cols(w_in, QL + np.arange(KVL), 128),
                    win_kr=tile_cols(w_in, QL + KVL + np.arange(64), 64)[0], wuq=tile_cols(w_uq, np.arange(MH * 192), 192),
                    wukv=tile_cols(w_ukv, np.arange(MH * 256), 256), wo=tile_cols(w_o, np.arange(D), 128))

    sh = dict(consts=consts, vecs_in=np.concatenate([fnorm(0, 0), mnorm(0)], 1))
    sh.update(ffn(0, 0, "A"))
    pc = [dict(xT=np.ascontiguousarray(x[b, h * TOK:(h + 1) * TOK, :].T)) for (b, h) in core]
    r = launch(cfg, "first", sh, pc)
    xo = [np.asarray(q["xo"]) for q in r]
    ho = [np.asarray(q["ho"]) for q in r]
    del sh
    sh = dict(consts=consts, vecs_in=np.concatenate([rv, vec_cols(f32(inp["mla0_q_norm"])), vec_cols(f32(inp["mla0_kv_norm"])),
                                                     fnorm(0, 1), fnorm(1, 0), mnorm(1)], 1))
    sh.update(mla_w("mla0_")); sh.update(ffn(0, 1, "A")); sh.update(ffn(1, 0, "B"))
    r = launch(cfg, "mla", sh, attn_percore(xo, ho))
    xo = [np.asarray(q["xo"]) for q in r]
    ho = [np.asarray(q["ho"]) for q in r]
    del sh
    w_in = f32(inp["dsa1_w_in"])
    DH, DG, IH = cfg.DH, cfg.DG, cfg.IH
    o0 = 0
    cq = np.arange(DH * 128); o0 += DH * 128
    ck = o0 + np.arange(DG * 128); o0 += DG * 128
    cv = o0 + np.arange(DG * 128); o0 += DG * 128
    cqi = o0 + np.arange(IH * 128); o0 += IH * 128
    cki = o0 + np.arange(128); o0 += 128
    cwi = o0 + np.arange(IH)
    sh = dict(consts=consts, vecs_in=np.concatenate([rv, fnorm(1, 1), fnorm(2, 0), mnorm(2)], 1),
              wq=tile_cols(w_in, cq, 128), wk=tile_cols(w_in, ck, 128), wv=tile_cols(w_in, cv, 256),
              wqi=tile_cols(w_in, cqi, 128), wki=tile_cols(w_in, cki, 128)[0], wwi=tile_cols(w_in, cwi, IH)[0],
              wo=tile_cols(f32(inp["dsa1_w_o"]), np.arange(D), 128))
    sh.update(ffn(1, 1, "A")); sh.update(ffn(2, 0, "B"))
    r = launch(cfg, "dsa", sh, attn_percore(xo, ho))
    xo = [np.asarray(q["xo"]) for q in r]
    ho = [np.asarray(q["ho"]) for q in r]
    del sh, w_in
    RC = D // 2
    RCB = RC // 128
    GL = cfg.GL
    ng = (GL + 127) // 128
    g = lambda k: f32(inp["rwkv2_" + k])
    mu = g("mu")
    g2 = g("g2")
    g2p = np.zeros((ng * 128, D), np.float32)
    g2p[:GL] = g2
    sh = dict(consts=consts, w1=tile_cols(g("w1"), np.arange(128), 128)[0], a1=tile_cols(g("a1"), np.arange(128), 128)[0],
              g1=tile_cols(g("g1"), np.arange((GL // 128) * 128), 128), g1b=tile_cols(g("g1"), (GL // 128) * 128 + np.arange(GL % 128), GL % 128)[0])
    hf = full_h(ho)
    halves = []
    for hh in range(2):
        own = hh * RC + np.arange(RC)
        vec = np.concatenate([vec_cols(mu[i]) for i in range(6)] +
                             [vec_cols(g(k)[own]) for k in ("w0", "a0", "k_k", "k_a", "lnx_w", "lnx_b")] + [vec_cols(g("r_k").reshape(-1)[own])], 1)
        halves.append(dict(wr=tile_cols(g("w_r"), own, 128), wk=tile_cols(g("w_k"), own, 128), wv=tile_cols(g("w_v"), own, 128),
                           w2=tile_cols(g("w2"), own, 128), a2=tile_cols(g("a2"), own, 128),
                           g2=np.ascontiguousarray(g2p[:, own].reshape(ng, 128, RCB, 128).transpose(2, 1, 0, 3)).reshape(RCB, 128, ng * 128),
                           vecs_in=vec))
    pc = []
    for c, (b, h) in enumerate(core):
        m = dict(halves[h]); m["hT_full"] = hf[b]
        pc.append(m)
    r = launch(cfg, "rwkv", sh, pc)
    yg = [np.asarray(q["ygT"]) for q in r]
    del sh, halves, pc
    ygf = [np.concatenate([yg[2 * b], yg[2 * b + 1]], axis=0) for b in range(B)]
    sh = dict(consts=consts, vecs_in=np.concatenate([fnorm(2, 1), fnorm(3, 0), mnorm(3)], 1), wo=tile_cols(g("w_o"), np.arange(D), 128))
    sh.update(ffn(2, 1, "A")); sh.update(ffn(3, 0, "B"))
    pc = [dict(xT=xo[c], yg=np.ascontiguousarray(ygf[b][:, h * TOK:(h + 1) * TOK])) for c, (b, h) in enumerate(core)]
    r = launch(cfg, "post", sh, pc)
    xo = [np.asarray(q["xo"]) for q in r]
    ho = [np.asarray(q["ho"]) for q in r]
    del sh
    sh = dict(consts=consts, vecs_in=np.concatenate([rv, vec_cols(f32(inp["mla3_q_norm"])), vec_cols(f32(inp["mla3_kv_norm"])),
                                                     fnorm(3, 1), vec_cols(f32(inp["final_norm"]))], 1))
    sh.update(mla_w("mla3_")); sh.update(ffn(3, 1, "A"))
    r = launch(cfg, "mla_last", sh, attn_percore(xo, ho))
    out = np.empty((B, SEQ, D), np.float32)
    for c, (b, h) in enumerate(core):
        out[b, h * TOK:(h + 1) * TOK, :] = np.asarray(r[c]["xo"]).T
    return out


def kernel(**inputs):
    return run_model(CFG, inputs)
```

```python
import math
from contextlib import ExitStack
import numpy as np
import ml_dtypes
import concourse.bass as bass
import concourse.mybir as mybir
from concourse.bass_utils import run_bass_kernel_spmd

F32 = mybir.dt.float32
BF16 = mybir.dt.bfloat16
I32 = mybir.dt.int32
ALU = mybir.AluOpType
AF = mybir.ActivationFunctionType
AX = mybir.AxisListType

NORM_EPS = 1e-6
EPOCH = 12000
NEG = -30000.0
_UN = [0]


def un(n):
    _UN[0] += 1
    return "%s__%d" % (n, _UN[0])


class Op:
    __slots__ = ("eng", "fn", "deps", "signaled", "count", "is_dma", "dsem", "dval")

    def __init__(self, eng, fn):
        self.eng = eng
        self.fn = fn
        self.deps = []
        self.signaled = False
        self.is_dma = False
        self.dsem = None
        self.dval = 0
        self.count = None


class Buf:
    __slots__ = ("w", "r")

    def __init__(self):
        self.w = None
        self.r = []


class Prog:
    ENGS = ("pe", "act", "dve", "pool", "sp")

    def __init__(self, nc):
        self.nc = nc
        self.ops = {e: [] for e in self.ENGS}
        self.bufs = {}
        self.dma_sems = {}

    def buf(self, name):
        b = self.bufs.get(name)
        if b is None:
            b = self.bufs[name] = Buf()
        return b

    def _collect(self, op, reads, writes):
        deps = []
        for n in reads:
            b = self.buf(n)
            if b.w is not None:
                deps.append(b.w)
        for n in writes:
            b = self.buf(n)
            if b.w is not None:
                deps.append(b.w)
            deps.extend(b.r)
        for d in deps:
            if d is op:
                continue
            if d.is_dma or d.eng != op.eng or op.eng != "pe":
                op.deps.append(d)
                if not d.is_dma:
                    d.signaled = True
        for n in reads:
            b = self.buf(n)
            if op.is_dma:
                b.r.append(op)
            else:
                b.r = [o for o in b.r if o.is_dma or o.eng != op.eng]
                b.r.append(op)
        for n in writes:
            b = self.buf(n)
            b.w = op
            b.r = []

    def emit(self, eng, fn, reads=(), writes=()):
        op = Op(eng, fn)
        self._collect(op, reads, writes)
        self.ops[eng].append(op)
        return op

    def dma(self, eng, out, in_, semkey, reads=(), writes=(), grp=None, **kw):
        ent = self.dma_sems.get(semkey)
        if ent is None:
            ent = self.dma_sems[semkey] = [self.nc.alloc_semaphore(name="d_" + semkey), 0, None, None, []]
        sem = ent[0]
        ent[1] += 16
        val = ent[1]

        def fn(e, out=out, in_=in_, sem=sem, kw=kw):
            return e.dma_start(out=out, in_=in_, **kw).then_inc(sem, 16)

        op = Op(eng, fn)
        op.is_dma = True
        op.dsem = sem
        op.dval = val
        same = grp is not None and ent[3] == grp
        if ent[2] is not None and not same:
            op.deps.append(ent[2])
        if same:
            for o in ent[4]:
                o.dval = val
            ent[4].append(op)
        else:
            ent[3] = grp
            ent[4] = [op]
        self._collect(op, reads, writes)
        ent[2] = op
        self.ops[eng].append(op)
        return op

    def barrier(self):
        lasts = []
        for e in self.ENGS:
            for o in reversed(self.ops[e]):
                if not o.is_dma and o.fn is not None:
                    lasts.append(o)
                    break
        dmas = [ent[2] for ent in self.dma_sems.values() if ent[2] is not None]
        for e in self.ENGS:
            op = Op(e, None)
            for d in lasts:
                if d.eng != e:
                    op.deps.append(d)
                    d.signaled = True
            op.deps.extend(dmas)
            self.ops[e].append(op)
        for b in self.bufs.values():
            b.w = None
            b.r = []

    def finalize(self):
        nc = self.nc
        sems = {e: [] for e in self.ENGS}
        for e in self.ENGS:
            c = 0
            for op in self.ops[e]:
                if op.is_dma or op.fn is None:
                    continue
                if op.signaled:
                    c += 1
                    ep = (c - 1) // EPOCH
                    while len(sems[e]) <= ep:
                        sems[e].append(nc.alloc_semaphore(name="s_%s_%d" % (e, len(sems[e]))))
                    op.count = (sems[e][ep], c - ep * EPOCH)
        hmap = {"pe": "tensor", "act": "scalar", "dve": "vector", "pool": "gpsimd", "sp": "sync"}
        stats = {}
        with nc.Block() as block:
            for e in self.ENGS:
                ops = self.ops[e]

                def body(h, ops=ops, e=e):
                    known = {}
                    nw = 0
                    for op in ops:
                        need = {}
                        for d in op.deps:
                            if d.is_dma:
                                s, v = d.dsem, d.dval
                            else:
                                s, v = d.count
                            k = s.num
                            if known.get(k, 0) >= v:
                                continue
                            if k not in need or need[k][1] < v:
                                need[k] = (s, v)
                        for k, (s, v) in need.items():
                            known[k] = v
                            h.wait_ge(s, v)
                            nw += 1
                        if op.fn is None:
                            continue
                        ins = op.fn(h)
                        if (not op.is_dma) and op.signaled:
                            ins.then_inc(op.count[0], 1)
                    stats[e] = (len(ops), nw)

                getattr(block, hmap[e])(body)
        return stats


class Ctx:
    def __init__(self, nc, D, TOK, consts_ap):
        self.nc = nc
        self.P = Prog(nc)
        self.D = D
        self.DC = D // 128
        self.TOK = TOK
        self.NT = TOK // 512
        self.cnt = {}
        P = self.P
        self.psS = nc.alloc_psum_tensor("psS", [128, 2048], F32)
        self.psum = [self.psS[:, i * 512:(i + 1) * 512] for i in range(4)]
        self.psum += [nc.alloc_psum_tensor("ps%d" % i, [128, 512], F32)[:] for i in range(4, 8)]
        self.ones = nc.alloc_sbuf_tensor("ones", [128, 128], F32)
        self.eps = nc.alloc_sbuf_tensor("eps", [128, 1], F32)
        self.cf = nc.alloc_sbuf_tensor("cf", [128, 3, 128], F32)
        self.identb = nc.alloc_sbuf_tensor("identb", [128, 128], BF16)
        P.emit("pool", lambda e: e.memset(self.ones[:], 1.0), writes=["ones"])
        P.emit("pool", lambda e: e.memset(self.eps[:], NORM_EPS), writes=["eps"])
        P.dma("sp", self.cf[:], consts_ap, "cf", writes=["cf"])
        P.emit("dve", lambda e: e.tensor_copy(out=self.identb[:], in_=self.cf[:, 0, :]), reads=["cf"], writes=["identb"])

    def basics(self, es, wcols=8192, nw=2):
        nc, TOK = self.nc, self.TOK
        self.wbuf = [es.enter_context(nc.sbuf_tensor(un("w%d" % i), [128, wcols], BF16)) for i in range(nw)]
        self.nw = nw
        self.xc = [es.enter_context(nc.sbuf_tensor(un("xc%d" % i), [128, TOK], F32)) for i in range(2)]
        self.sg = [es.enter_context(nc.sbuf_tensor(un("sg%d" % i), [128, TOK], F32)) for i in range(2)]
        self.hout = [es.enter_context(nc.sbuf_tensor(un("hout%d" % i), [128, TOK], BF16)) for i in range(2)]
        self.rstd = es.enter_context(nc.sbuf_tensor(un("rstd"), [128, TOK], F32))

    def rot(self, key, n):
        v = self.cnt.get(key, 0)
        self.cnt[key] = v + 1
        return v % n

    def load_w(self, dram_ap, ncols, shape3=None):
        s = self.rot("w", self.nw)
        dst = self.wbuf[s][:, 0:ncols]
        if shape3 is not None:
            dst = dst.rearrange(shape3[0], **shape3[1])
        self.P.dma("pool", dst, dram_ap, "w%d" % s, writes=["w%d" % s])
        return self.wbuf[s], "w%d" % s


def dview(ap):
    return ap.rearrange("(c p) t -> c p t", p=128)


def emit_stats(cx, src_v, nchunks, inv_n, dkey, load=True, src_sb=None):
    xc_, sg_, rstd_, hout_ = cx.xc, cx.sg, cx.rstd, cx.hout
    P, TOK, NT = cx.P, cx.TOK, cx.NT
    pb = cx.rot("pb", 2) * 4
    for kc in range(nchunks):
        if src_sb is None:
            s = cx.rot("xc", 2)
            P.dma("sp", xc_[s][:], src_v[kc], "xc%d" % s, reads=["%s_%d" % (dkey, kc)], writes=["xc%d" % s])
            xin, xname = xc_[s][:], "xc%d" % s
        else:
            xin, xname = src_sb(kc)
        q = cx.rot("sg", 2)
        P.emit("act", lambda e, q=q, xin=xin: e.activation(out=sg_[q][:, 0:TOK], in_=xin, func=AF.Square),
               reads=[xname], writes=["sg%d" % q])
        for th in range(NT):
            P.emit("pe", lambda e, q=q, th=th, kc=kc: e.matmul(cx.psum[pb + th][:], cx.ones[:], sg_[q][:, th * 512:(th + 1) * 512],
                                                           start=(kc == 0), stop=(kc == nchunks - 1)),
                   reads=["sg%d" % q, "ones"], writes=["ps%d" % (pb + th)])
    for th in range(NT):
        P.emit("act", lambda e, th=th: e.activation(out=rstd_[:, th * 512:(th + 1) * 512], in_=cx.psum[pb + th][:], func=AF.Sqrt,
                                                    bias=cx.eps[:, 0:1], scale=inv_n),
               reads=["ps%d" % (pb + th), "eps"], writes=["rstd"])
    P.emit("dve", lambda e: e.reciprocal(out=rstd_[:, 0:TOK], in_=rstd_[:, 0:TOK]), reads=["rstd"], writes=["rstd"])


def emit_norm(cx, src_ap, dkey, gain, gname, hT=None, out_dram=None, out_dt=BF16, okey=None):
    xc_, sg_, rstd_, hout_ = cx.xc, cx.sg, cx.rstd, cx.hout
    P, TOK, DC = cx.P, cx.TOK, cx.DC
    src_v = dview(src_ap)
    emit_stats(cx, src_v, DC, 1.0 / cx.D, dkey)
    for kc in range(DC):
        s = cx.rot("xc", 2)
        P.dma("sp", xc_[s][:], src_v[kc], "xc%d" % s, reads=["%s_%d" % (dkey, kc)], writes=["xc%d" % s])
        if hT is not None:
            P.emit("dve", lambda e, s=s, kc=kc: e.scalar_tensor_tensor(out=hT[:, kc, :], in0=xc_[s][:], scalar=gain[:, kc:kc + 1],
                                                                     in1=rstd_[:, 0:TOK], op0=ALU.mult, op1=ALU.mult),
                   reads=["xc%d" % s, "rstd", gname], writes=["hT_%d" % kc])
        if out_dram is not None:
            ov = dview(out_dram)
            q = cx.rot("sg", 2)
            if out_dt == BF16:
                dst = hout_[q][:, 0:TOK]
                dname = "hout%d" % q
            else:
                dst = sg_[q][:, 0:TOK]
                dname = "sg%d" % q
            P.emit("dve", lambda e, s=s, kc=kc, dst=dst: e.scalar_tensor_tensor(out=dst, in0=xc_[s][:], scalar=gain[:, kc:kc + 1],
                                                                              in1=rstd_[:, 0:TOK], op0=ALU.mult, op1=ALU.mult),
                   reads=["xc%d" % s, "rstd", gname], writes=[dname])
            P.dma("sp", ov[kc], dst, "ho%d" % q, reads=[dname], writes=["%s_%d" % (okey, kc)])


def emit_down(cx, aT, aname, KC, wslab, src_ap, skey, dst_ap, dkey, scale):
    xc_, sg_, rstd_, hout_ = cx.xc, cx.sg, cx.rstd, cx.hout
    P, TOK, NT, DC = cx.P, cx.TOK, cx.NT, cx.DC
    sv, dv = dview(src_ap), dview(dst_ap)
    for i in range(DC):
        wt, wname = cx.load_w(wslab(i), KC * 128)
        s = cx.rot("xc", 2)
        P.dma("sp", xc_[s][:], sv[i], "xc%d" % s, reads=["%s_%d" % (skey, i)], writes=["xc%d" % s])
        pb = cx.rot("pb", 2) * 4
        for fc in range(KC):
            for th in range(NT):
                P.emit("pe", lambda e, wt=wt, th=th, fc=fc, pb=pb: e.matmul(
                    cx.psum[pb + th][:], wt[:, fc * 128:(fc + 1) * 128], aT[:, fc, th * 512:(th + 1) * 512],
                    start=(fc == 0), stop=(fc == KC - 1)),
                    reads=[wname, "%s_%d" % (aname, fc)], writes=["ps%d" % (pb + th)])
        for th in range(NT):
            P.emit("dve", lambda e, s=s, th=th, pb=pb: e.scalar_tensor_tensor(
                out=xc_[s][:, th * 512:(th + 1) * 512], in0=cx.psum[pb + th][:], scalar=scale, in1=xc_[s][:, th * 512:(th + 1) * 512],
                op0=ALU.mult, op1=ALU.add), reads=["ps%d" % (pb + th), "xc%d" % s], writes=["xc%d" % s])
        P.dma("sp", dv[i], xc_[s][:], "xo%d" % s, reads=["xc%d" % s], writes=["%s_%d" % (dkey, i)])


def emit_ffn(cx, src_ap, skey, dst_ap, dkey, wgu, wd, gain, gname, hT, aT, F, NSPLIT):
    xc_, sg_, rstd_, hout_ = cx.xc, cx.sg, cx.rstd, cx.hout
    P, TOK, NT, DC = cx.P, cx.TOK, cx.NT, cx.DC
    FB = F // 128
    FH = FB // NSPLIT
    emit_norm(cx, src_ap, skey, gain, gname, hT=hT)
    for sp in range(NSPLIT):
        for jj in range(FH):
            j = sp * FH + jj
            wt, wname = cx.load_w(wgu[j].rearrange("t p n -> p t n"), 2 * DC * 128, ("p (t n) -> p t n", dict(t=2)))
            pb = cx.rot("pb", 2) * 4
            for kc in range(DC):
                for g in range(2):
                    for th in range(NT):
                        P.emit("pe", lambda e, wt=wt, g=g, th=th, kc=kc, pb=pb: e.matmul(
                            cx.psum[pb + g * 2 + th][:], wt[:, (g * DC + kc) * 128:(g * DC + kc + 1) * 128],
                            hT[:, kc, th * 512:(th + 1) * 512], start=(kc == 0), stop=(kc == DC - 1)),
                            reads=[wname, "hT_%d" % kc], writes=["ps%d" % (pb + g * 2 + th)])
            for th in range(NT):
                q = cx.rot("sg", 2)
                P.emit("act", lambda e, q=q, th=th, pb=pb: e.activation(out=sg_[q][:, 0:512], in_=cx.psum[pb + th][:], func=AF.Silu),
                       reads=["ps%d" % (pb + th)], writes=["sg%d" % q])
                P.emit("dve", lambda e, q=q, th=th, pb=pb, jj=jj: e.tensor_tensor(
                    out=aT[:, jj, th * 512:(th + 1) * 512], in0=cx.psum[pb + 2 + th][:], in1=sg_[q][:, 0:512], op=ALU.mult),
                    reads=["ps%d" % (pb + 2 + th), "sg%d" % q], writes=["aT_%d" % jj])
        emit_down(cx, aT, "aT", FH, lambda i, sp=sp: wd[sp, i], src_ap if sp == 0 else dst_ap, skey if sp == 0 else dkey,
                  dst_ap, dkey, 0.5)


TWO_PI = 2.0 * math.pi


def emit_rope_tables(cx, pos_ap, n, invf, sgn, cosT, sinT, tname):
    P, nc = cx.P, cx.nc
    with ExitStack() as es:
        posi = es.enter_context(nc.sbuf_tensor(un("rt_posi"), [128, n], I32))
        ang = es.enter_context(nc.sbuf_tensor(un("rt_ang"), [128, n], F32))
        ki = es.enter_context(nc.sbuf_tensor(un("rt_ki"), [128, n], I32))
        tf = es.enter_context(nc.sbuf_tensor(un("rt_tf"), [128, n], F32))
        P.dma("sp", posi[:], pos_ap.to_broadcast([128, n]), "rtp", writes=["rt_posi"])
        P.emit("dve", lambda e: e.tensor_copy(out=ang[:], in_=posi[:]), reads=["rt_posi"], writes=["rt_ang"])
        P.emit("dve", lambda e: e.tensor_scalar(out=ang[:], in0=ang[:], scalar1=invf, scalar2=None, op0=ALU.mult),
               reads=["rt_ang", "vecs"], writes=["rt_ang"])
        for phase, dst, dname, sc in ((0.0, sinT, tname + "_sin", sgn), (0.5 * math.pi, cosT, tname + "_cos", 1.0)):
            P.emit("dve", lambda e, phase=phase: e.tensor_scalar(out=ki[:], in0=ang[:], scalar1=1.0 / TWO_PI, scalar2=phase / TWO_PI,
                                                                op0=ALU.mult, op1=ALU.add), reads=["rt_ang"], writes=["rt_ki"])
            P.emit("dve", lambda e: e.tensor_copy(out=tf[:], in_=ki[:]), reads=["rt_ki"], writes=["rt_tf"])
            P.emit("dve", lambda e: e.scalar_tensor_tensor(out=tf[:], in0=tf[:], scalar=-TWO_PI, in1=ang[:], op0=ALU.mult, op1=ALU.add),
                   reads=["rt_tf", "rt_ang"], writes=["rt_tf"])
            P.emit("dve", lambda e, phase=phase: e.tensor_scalar(out=tf[:], in0=tf[:], scalar1=phase, scalar2=math.pi, op0=ALU.add, op1=ALU.min),
                   reads=["rt_tf"], writes=["rt_tf"])
            P.emit("dve", lambda e: e.tensor_scalar(out=tf[:], in0=tf[:], scalar1=-math.pi, scalar2=None, op0=ALU.max),
                   reads=["rt_tf"], writes=["rt_tf"])
            P.emit("act", lambda e, dst=dst, sc=sc: e.activation(out=dst[:, 0:n], in_=tf[:], func=AF.Sin, scale=sc),
                   reads=["rt_tf", "vecs"], writes=[dname])
        P.barrier()


def emit_rope(cx, src_ps, sname, R, n, cos_ap, sin_ap, tnames, perm, dst_ap, dname):
    P = cx.P
    q = cx.rot("ropet", 2)
    ra, rb, rc = cx.ra[q], cx.rb[q], cx.rc[q]
    pz = 4 + cx.rot("pb2", 2)
    P.emit("act", lambda e: e.activation(out=ra[0:R, 0:n], in_=src_ps, func=AF.Copy), reads=[sname], writes=["ra%d" % q])
    P.emit("pe", lambda e: e.matmul(cx.psum[pz][0:R, 0:n], perm, ra[0:R, 0:n], start=True, stop=True),
           reads=["ra%d" % q, "cf"], writes=["ps%d" % pz])
    P.emit("dve", lambda e: e.tensor_tensor(out=rb[0:R, 0:n], in0=ra[0:R, 0:n], in1=cos_ap, op=ALU.mult),
           reads=["ra%d" % q, tnames[0]], writes=["rb%d" % q])
    P.emit("dve", lambda e: e.tensor_tensor(out=rc[0:R, 0:n], in0=cx.psum[pz][0:R, 0:n], in1=sin_ap, op=ALU.mult),
           reads=["ps%d" % pz, tnames[1]], writes=["rc%d" % q])
    P.emit("dve", lambda e: e.tensor_tensor(out=dst_ap, in0=rb[0:R, 0:n], in1=rc[0:R, 0:n], op=ALU.add),
           reads=["rb%d" % q, "rc%d" % q], writes=[dname])


def alloc_rope_tmps(cx, es):
    nc = cx.nc
    cx.ra = [es.enter_context(nc.sbuf_tensor(un("ra%d" % i), [128, 512], F32)) for i in range(2)]
    cx.rb = [es.enter_context(nc.sbuf_tensor(un("rb%d" % i), [128, 512], F32)) for i in range(2)]
    cx.rc = [es.enter_context(nc.sbuf_tensor(un("rc%d" % i), [128, 512], F32)) for i in range(2)]


def alloc_attn(cx, es, TT):
    nc = cx.nc
    cx.pexp = [es.enter_context(nc.sbuf_tensor(un("pexp%d" % i), [128, TT], BF16)) for i in range(2)]
    cx.PT = [es.enter_context(nc.sbuf_tensor(un("PT%d" % i), [128, TT], BF16)) for i in range(2)]
    cx.mk = [es.enter_context(nc.sbuf_tensor(un("mk%d" % i), [128, TT], BF16)) for i in range(2)]
    cx.otm = [es.enter_context(nc.sbuf_tensor(un("otm%d" % i), [128, 128], BF16)) for i in range(2)]
    cx.sm = [es.enter_context(nc.sbuf_tensor(un("sm%d" % i), [128, 4], F32)) for i in range(2)]


def emit_attn_tile(cx, TT, qk_pairs, vfn, mask_ap, mname, scale, odst, oname):
    P = cx.P
    NKC = TT // 512
    for c in range(NKC):
        for idx, (l, ln, r, rn) in enumerate(qk_pairs(c)):
            P.emit("pe", lambda e, c=c, l=l, r=r, idx=idx: e.matmul(cx.psum[c][:], l, r, start=(idx == 0), stop=False),
                   reads=[ln, rn], writes=["ps%d" % c])
        P.emit("pe", lambda e, c=c: e.matmul(cx.psum[c][:], cx.identb[:], mask_ap[:, c * 512:(c + 1) * 512], start=False, stop=True),
               reads=["identb", mname], writes=["ps%d" % c])
    sname = ["ps%d" % c for c in range(NKC)]
    q = cx.rot("attn", 2)
    sm, pexp, PT, otm = cx.sm[q], cx.pexp[q], cx.PT[q], cx.otm[q]
    P.emit("dve", lambda e: e.tensor_reduce(out=sm[:, 0:1], in_=cx.psS[:, 0:TT], axis=AX.X, op=ALU.max), reads=sname, writes=["sm%d" % q])
    P.emit("dve", lambda e: e.tensor_scalar(out=sm[:, 1:2], in0=sm[:, 0:1], scalar1=-scale, scalar2=None, op0=ALU.mult),
           reads=["sm%d" % q], writes=["sm%d" % q])
    P.emit("act", lambda e: e.activation(out=pexp[:, 0:TT], in_=cx.psS[:, 0:TT], func=AF.Exp, bias=sm[:, 1:2], scale=scale,
                                         accum_out=sm[:, 2:3]), reads=sname + ["sm%d" % q], writes=["pexp%d" % q, "sm%d" % q])
    P.emit("dve", lambda e: e.reciprocal(out=sm[:, 3:4], in_=sm[:, 2:3]), reads=["sm%d" % q], writes=["sm%d" % q])
    ps6b = cx.psum[6].bitcast(BF16)
    for g in range(NKC):
        hf = cx.rot("pt", 2)
        for j in range(4):
            kb = g * 4 + j
            P.emit("pe", lambda e, hf=hf, j=j, kb=kb: e.transpose(out=ps6b[:, hf * 512 + j * 128: hf * 512 + (j + 1) * 128],
                                                                 in_=pexp[:, kb * 128:(kb + 1) * 128], identity=cx.identb[:]),
                   reads=["pexp%d" % q, "identb"], writes=["ps6"])
        if g % 2 == 0:
            P.emit("act", lambda e, hf=hf, g=g: e.activation(out=PT[:, g * 512:(g + 1) * 512], in_=ps6b[:, hf * 512:(hf + 1) * 512], func=AF.Copy),
                   reads=["ps6"], writes=["PT%d_%d" % (q, g)])
        else:
            P.emit("dve", lambda e, hf=hf, g=g: e.tensor_copy(out=PT[:, g * 512:(g + 1) * 512], in_=ps6b[:, hf * 512:(hf + 1) * 512]),
                   reads=["ps6"], writes=["PT%d_%d" % (q, g)])
    nkb = TT // 128
    for kb in range(nkb):
        v, vn = vfn(kb)
        P.emit("pe", lambda e, kb=kb, v=v: e.matmul(cx.psum[7][:, 0:128], PT[:, kb * 128:(kb + 1) * 128], v, start=(kb == 0), stop=(kb == nkb - 1)),
               reads=["PT%d_%d" % (q, kb // 4), vn], writes=["ps7"])
    P.emit("act", lambda e: e.activation(out=otm[:], in_=cx.psum[7][:, 0:128], func=AF.Copy, scale=sm[:, 3:4]),
           reads=["ps7", "sm%d" % q], writes=["otm%d" % q])
    ps7b = cx.psum[7].bitcast(BF16)
    P.emit("pe", lambda e: e.transpose(out=ps7b[:, 512:640], in_=otm[:], identity=cx.identb[:]), reads=["otm%d" % q, "identb"], writes=["ps7"])
    P.emit("dve", lambda e: e.tensor_copy(out=odst, in_=ps7b[:, 512:640]), reads=["ps7"], writes=[oname])


def emit_proj(cx, ps_ap, psname, KC, lhs_fn, rhs_fn):
    for kc in range(KC):
        l, ln = lhs_fn(kc)
        r, rn = rhs_fn(kc)
        cx.P.emit("pe", lambda e, l=l, r=r, kc=kc: e.matmul(ps_ap, l, r, start=(kc == 0), stop=(kc == KC - 1)),
                  reads=[ln, rn], writes=[psname])


def emit_evac(cx, dst, dname, src, sname, k):
    if k % 2 == 0:
        cx.P.emit("act", lambda e: e.activation(out=dst, in_=src, func=AF.Copy), reads=[sname], writes=[dname])
    else:
        cx.P.emit("dve", lambda e: e.tensor_copy(out=dst, in_=src), reads=[sname], writes=[dname])


def load_hTc(cx, hTc, src_ap, c0, W=512):
    v = src_ap.rearrange("(c p) t -> p c t", p=128)
    cx.P.dma("sp", hTc[:, :, 0:W], v[:, :, c0:c0 + W], "hTc", writes=["hTc"])


def emit_mla(cx, io, MH, QL, KVL, TT):
    P, nc, TOK, DC = cx.P, cx.nc, cx.TOK, cx.DC
    QC, KVC = QL // 128, KVL // 128
    NQT = TOK // 128
    scale = (128 + 64) ** -0.5
    vecs = io["vecs"]
    perm64 = cx.cf[0:64, 1, 0:64]
    with ExitStack() as esP:
        ckvn = esP.enter_context(nc.sbuf_tensor(un("ckvn"), [128, KVC, TT], BF16))
        cqn = esP.enter_context(nc.sbuf_tensor(un("cqn"), [128, QC, TOK], BF16))
        krT = esP.enter_context(nc.sbuf_tensor(un("krT"), [128, TT], BF16))
        cosQ = esP.enter_context(nc.sbuf_tensor(un("cosQ"), [128, TOK], F32))
        sinQ = esP.enter_context(nc.sbuf_tensor(un("sinQ"), [128, TOK], F32))
        emit_rope_tables(cx, io["pos_own"], TOK, vecs[:, 0:1], vecs[:, 1:2], cosQ, sinQ, "tq")
        with ExitStack() as es:
            cosK = es.enter_context(nc.sbuf_tensor(un("cosK"), [128, TT], F32))
            sinK = es.enter_context(nc.sbuf_tensor(un("sinK"), [128, TT], F32))
            emit_rope_tables(cx, io["pos_full"], TT, vecs[:, 0:1], vecs[:, 1:2], cosK, sinK, "tk")
            cx.basics(es, wcols=DC * 128, nw=3)
            rstd_ = cx.rstd
            alloc_rope_tmps(cx, es)
            hTc = es.enter_context(nc.sbuf_tensor(un("hTc"), [128, DC, 512], BF16))
            latf = es.enter_context(nc.sbuf_tensor(un("latf"), [128, max(QC, KVC), 512], F32))
            sv_TOK, sv_NT = cx.TOK, cx.NT
            cx.TOK, cx.NT = 512, 1
            for tc in range(TT // 512):
                load_hTc(cx, hTc, io["hT_full"], tc * 512)
                for blk in range(KVC):
                    wt, wn = cx.load_w(io["win_kv"][blk], DC * 128)
                    pz = 4 + cx.rot("pb2", 2)
                    emit_proj(cx, cx.psum[pz][:], "ps%d" % pz, DC, lambda kc, wt=wt, wn=wn: (wt[:, kc * 128:(kc + 1) * 128], wn),
                              lambda kc: (hTc[:, kc, :], "hTc"))
                    emit_evac(cx, latf[:, blk, :], "latf_%d" % blk, cx.psum[pz][:], "ps%d" % pz, 0)
                wt, wn = cx.load_w(io["win_kr"], DC * 64)
                pz = 4 + cx.rot("pb2", 2)
                emit_proj(cx, cx.psum[pz][0:64, :], "ps%d" % pz, DC, lambda kc, wt=wt, wn=wn: (wt[:, kc * 64:(kc + 1) * 64], wn),
                          lambda kc: (hTc[:, kc, :], "hTc"))
                emit_rope(cx, cx.psum[pz][0:64, :], "ps%d" % pz, 64, 512, cosK[0:64, tc * 512:(tc + 1) * 512], sinK[0:64, tc * 512:(tc + 1) * 512],
                          ("tk_cos", "tk_sin"), perm64, krT[0:64, tc * 512:(tc + 1) * 512], "krT")
                emit_stats(cx, None, KVC, 1.0 / KVL, None, src_sb=lambda kc: (latf[:, kc, :], "latf_%d" % kc))
                for blk in range(KVC):
                    P.emit("dve", lambda e, blk=blk, tc=tc: e.scalar_tensor_tensor(
                        out=ckvn[:, blk, tc * 512:(tc + 1) * 512], in0=latf[:, blk, :], scalar=vecs[:, 4 + QC + blk:5 + QC + blk],
                        in1=rstd_[:, 0:512], op0=ALU.mult, op1=ALU.mult), reads=["latf_%d" % blk, "rstd", "vecs"], writes=["ckvn"])
            for tq in range(TOK_full(sv_TOK) // 512):
                load_hTc(cx, hTc, io["hT_own"], tq * 512)
                for blk in range(QC):
                    wt, wn = cx.load_w(io["win_q"][blk], DC * 128)
                    pz = 4 + cx.rot("pb2", 2)
                    emit_proj(cx, cx.psum[pz][:], "ps%d" % pz, DC, lambda kc, wt=wt, wn=wn: (wt[:, kc * 128:(kc + 1) * 128], wn),
                              lambda kc: (hTc[:, kc, :], "hTc"))
                    emit_evac(cx, latf[:, blk, :], "latf_%d" % blk, cx.psum[pz][:], "ps%d" % pz, blk)
                emit_stats(cx, None, QC, 1.0 / QL, None, src_sb=lambda kc: (latf[:, kc, :], "latf_%d" % kc))
                for blk in range(QC):
                    P.emit("dve", lambda e, blk=blk, tq=tq: e.scalar_tensor_tensor(
                        out=cqn[:, blk, tq * 512:(tq + 1) * 512], in0=latf[:, blk, :], scalar=vecs[:, 4 + blk:5 + blk],
                        in1=rstd_[:, 0:512], op0=ALU.mult, op1=ALU.mult), reads=["latf_%d" % blk, "rstd", "vecs"], writes=["cqn"])
            cx.TOK, cx.NT = sv_TOK, sv_NT
            P.barrier()
        with ExitStack() as es:
            nwc = max(QC * 192, KVC * 256)
            cx.wbuf = [es.enter_context(nc.sbuf_tensor(un("w%d" % i), [128, nwc], BF16)) for i in range(4)]
            cx.nw = 4
            alloc_rope_tmps(cx, es)
            alloc_attn(cx, es, TT)
            knT = [es.enter_context(nc.sbuf_tensor(un("knT%d" % i), [128, TT], BF16)) for i in range(2)]
            Vh = [es.enter_context(nc.sbuf_tensor(un("Vh%d" % i), [128, TT], BF16)) for i in range(2)]
            qnT = [es.enter_context(nc.sbuf_tensor(un("qnT%d" % i), [128, TOK], BF16)) for i in range(2)]
            qrT = [es.enter_context(nc.sbuf_tensor(un("qrT%d" % i), [128, TOK], BF16)) for i in range(2)]
            oth = [es.enter_context(nc.sbuf_tensor(un("oth%d" % i), [128, TOK], BF16)) for i in range(2)]
            ov = io["oT"].rearrange("(h p) t -> h p t", p=128)
            ev = 0
            for h in range(MH):
                b = h % 2
                wq, wqn = cx.load_w(io["wuq"][h], QC * 192)
                wkv, wkvn = cx.load_w(io["wukv"][h], KVC * 256)
                for c in range(TT // 512):
                    pz = 4 + cx.rot("pb2", 2)
                    emit_proj(cx, cx.psum[pz][:], "ps%d" % pz, KVC, lambda kc: (wkv[:, kc * 256:kc * 256 + 128], wkvn),
                              lambda kc, c=c: (ckvn[:, kc, c * 512:(c + 1) * 512], "ckvn"))
                    emit_evac(cx, knT[b][:, c * 512:(c + 1) * 512], "knT%d" % b, cx.psum[pz][:], "ps%d" % pz, ev); ev += 1
                for t4 in range(TT // 512):
                    pz = 4 + cx.rot("pb2", 2)
                    for j in range(4):
                        t = t4 * 4 + j
                        emit_proj(cx, cx.psum[pz][:, j * 128:(j + 1) * 128], "ps%d" % pz, KVC,
                                  lambda kc, t=t: (ckvn[:, kc, t * 128:(t + 1) * 128], "ckvn"),
                                  lambda kc: (wkv[:, kc * 256 + 128:kc * 256 + 256], wkvn))
                    emit_evac(cx, Vh[b][:, t4 * 512:(t4 + 1) * 512], "Vh%d" % b, cx.psum[pz][:], "ps%d" % pz, ev); ev += 1
                for c in range(TOK // 512):
                    pz = 4 + cx.rot("pb2", 2)
                    emit_proj(cx, cx.psum[pz][:], "ps%d" % pz, QC, lambda kc: (wq[:, kc * 192:kc * 192 + 128], wqn),
                              lambda kc, c=c: (cqn[:, kc, c * 512:(c + 1) * 512], "cqn"))
                    emit_evac(cx, qnT[b][:, c * 512:(c + 1) * 512], "qnT%d" % b, cx.psum[pz][:], "ps%d" % pz, ev); ev += 1
                    pz = 4 + cx.rot("pb2", 2)
                    emit_proj(cx, cx.psum[pz][0:64, :], "ps%d" % pz, QC, lambda kc: (wq[:, kc * 192 + 128:kc * 192 + 192], wqn),
                              lambda kc, c=c: (cqn[:, kc, c * 512:(c + 1) * 512], "cqn"))
                    emit_rope(cx, cx.psum[pz][0:64, :], "ps%d" % pz, 64, 512, cosQ[0:64, c * 512:(c + 1) * 512], sinQ[0:64, c * 512:(c + 1) * 512],
                              ("tq_cos", "tq_sin"), perm64, qrT[b][0:64, c * 512:(c + 1) * 512], "qrT%d" % b)
                for i in range(NQT):
                    m = cx.rot("mk", 2)
                    P.dma("sp", cx.mk[m][:], io["maskb"][i], "mk%d" % m, writes=["mk%d" % m])

                    def pairs(c, i=i, b=b):
                        return [(qnT[b][:, i * 128:(i + 1) * 128], "qnT%d" % b, knT[b][:, c * 512:(c + 1) * 512], "knT%d" % b),
                                (qrT[b][0:64, i * 128:(i + 1) * 128], "qrT%d" % b, krT[0:64, c * 512:(c + 1) * 512], "krT")]
                    emit_attn_tile(cx, TT, pairs, lambda kb, b=b: (Vh[b][:, kb * 128:(kb + 1) * 128], "Vh%d" % b), cx.mk[m], "mk%d" % m,
                                   scale, oth[b][:, i * 128:(i + 1) * 128], "oth%d" % b)
                P.dma("sp", ov[h], oth[b][:], "oth%d" % b, reads=["oth%d" % b], writes=["oT_%d" % h])
            P.barrier()


def TOK_full(x):
    return x


ROPE_THETA = 10000.0


def tile_cols(w, col_idx, M):
    K = w.shape[0]
    KC = K // 128
    sub = w[:, col_idx]
    nb = sub.shape[1] // M
    return np.ascontiguousarray(sub.reshape(KC, 128, nb, M).transpose(2, 1, 0, 3)).reshape(nb, 128, KC * M)


def vec_cols(g):
    return np.ascontiguousarray(g.reshape(-1, 128).T)


def const_tables():
    cf = np.zeros((128, 3, 128), np.float32)
    p = np.arange(128)
    cf[p, 0, p] = 1.0
    s64 = np.where(p % 64 < 32, p + 32, p - 32)
    cf[s64, 1, p] = 1.0
    s128 = np.where(p < 64, p + 64, p - 64)
    cf[s128, 2, p] = 1.0
    return cf


def rope_vecs():
    p = np.arange(128)
    v = np.zeros((128, 4), np.float32)
    inv64 = np.power(ROPE_THETA, -np.arange(0, 64, 2, dtype=np.float32) / 64).astype(np.float32)
    inv128 = np.power(ROPE_THETA, -np.arange(0, 128, 2, dtype=np.float32) / 128).astype(np.float32)
    v[:64, 0] = inv64[p[:64] % 32]
    v[:, 1] = np.where(p % 64 < 32, -1.0, 1.0)
    v[:, 2] = inv128[p % 64]
    v[:, 3] = np.where(p < 64, -1.0, 1.0)
    return v


def causal_maskb(half, TOK, TT):
    q = half * TOK + np.arange(TOK)
    k = np.arange(TT)
    ok = (k[None, :] // 64) <= (q[:, None] // 64)
    m = np.where(ok, 0.0, NEG).astype(np.float32)
    return m.reshape(TOK // 128, 128, TT).astype(ml_dtypes.bfloat16)


def emit_dsa(cx, io, DH, DG, IH, TOPK, TT):
    P, nc, TOK, DC = cx.P, cx.nc, cx.TOK, cx.DC
    NQT = TOK // 128
    REP = DH // DG
    scale = 128 ** -0.5
    wscale = IH ** -0.5 * 128 ** -0.5
    vecs = io["vecs"]
    perm64 = cx.cf[:, 1, :]
    perm128 = cx.cf[:, 2, :]
    qscr = nc.dram_tensor(un("qscr"), [DH * 128, TOK], BF16).ap()
    qsv = qscr.rearrange("(h p) t -> h p t", p=128)
    with ExitStack() as esP:
        masks = esP.enter_context(nc.sbuf_tensor(un("masks"), [128, NQT, TT], BF16))
        with ExitStack() as esI:
            qiT = esI.enter_context(nc.sbuf_tensor(un("qiT"), [128, IH, TOK], BF16))
            kiT = esI.enter_context(nc.sbuf_tensor(un("kiT"), [128, TT], BF16))
            wiT = esI.enter_context(nc.sbuf_tensor(un("wiT"), [128, NQT, IH], F32))
            with ExitStack() as es:
                cosK = es.enter_context(nc.sbuf_tensor(un("cosK"), [128, TT], F32))
                sinK = es.enter_context(nc.sbuf_tensor(un("sinK"), [128, TT], F32))
                cosQ = es.enter_context(nc.sbuf_tensor(un("cosQ"), [128, TOK], F32))
                sinQ = es.enter_context(nc.sbuf_tensor(un("sinQ"), [128, TOK], F32))
                emit_rope_tables(cx, io["pos_full"], TT, vecs[:, 0:1], vecs[:, 1:2], cosK, sinK, "tk")
                emit_rope_tables(cx, io["pos_own"], TOK, vecs[:, 0:1], vecs[:, 1:2], cosQ, sinQ, "tq")
                cx.wbuf = [es.enter_context(nc.sbuf_tensor(un("w%d" % i), [128, DC * 128], BF16)) for i in range(3)]
                cx.nw = 3
                alloc_rope_tmps(cx, es)
                hTc = es.enter_context(nc.sbuf_tensor(un("hTc"), [128, DC, 512], BF16))
                for tc in range(TT // 512):
                    load_hTc(cx, hTc, io["hT_full"], tc * 512)
                    wt, wn = cx.load_w(io["wki"], DC * 128)
                    pz = 4 + cx.rot("pb2", 2)
                    emit_proj(cx, cx.psum[pz][:], "ps%d" % pz, DC, lambda kc, wt=wt, wn=wn: (wt[:, kc * 128:(kc + 1) * 128], wn),
                              lambda kc: (hTc[:, kc, :], "hTc"))
                    emit_rope(cx, cx.psum[pz][:], "ps%d" % pz, 128, 512, cosK[:, tc * 512:(tc + 1) * 512], sinK[:, tc * 512:(tc + 1) * 512],
                              ("tk_cos", "tk_sin"), perm64, kiT[:, tc * 512:(tc + 1) * 512], "kiT")
                for tq in range(TOK // 512):
                    load_hTc(cx, hTc, io["hT_own"], tq * 512)
                    for h in range(IH):
                        wt, wn = cx.load_w(io["wqi"][h], DC * 128)
                        pz = 4 + cx.rot("pb2", 2)
                        emit_proj(cx, cx.psum[pz][:], "ps%d" % pz, DC, lambda kc, wt=wt, wn=wn: (wt[:, kc * 128:(kc + 1) * 128], wn),
                                  lambda kc: (hTc[:, kc, :], "hTc"))
                        emit_rope(cx, cx.psum[pz][:], "ps%d" % pz, 128, 512, cosQ[:, tq * 512:(tq + 1) * 512], sinQ[:, tq * 512:(tq + 1) * 512],
                                  ("tq_cos", "tq_sin"), perm64, qiT[:, h, tq * 512:(tq + 1) * 512], "qiT_%d" % h)
                    wt, wn = cx.load_w(io["wwi"], DC * IH)
                    for j in range(4):
                        t = tq * 4 + j
                        pz = 4 + cx.rot("pb2", 2)
                        emit_proj(cx, cx.psum[pz][:, 0:IH], "ps%d" % pz, DC, lambda kc, j=j: (hTc[:, kc, j * 128:(j + 1) * 128], "hTc"),
                                  lambda kc, wt=wt, wn=wn: (wt[:, kc * IH:(kc + 1) * IH], wn))
                        P.emit("act", lambda e, t=t, pz=pz: e.activation(out=wiT[:, t, :], in_=cx.psum[pz][:, 0:IH], func=AF.Copy, scale=wscale),
                               reads=["ps%d" % pz], writes=["wiT"])
                P.barrier()
            with ExitStack() as es:
                acc = es.enter_context(nc.sbuf_tensor(un("acc"), [128, TT], F32))
                work = es.enter_context(nc.sbuf_tensor(un("work"), [128, TT], F32))
                rl = [es.enter_context(nc.sbuf_tensor(un("rl%d" % i), [128, 512], BF16)) for i in range(2)]
                mk = [es.enter_context(nc.sbuf_tensor(un("mk%d" % i), [128, TT], BF16)) for i in range(2)]
                m8 = es.enter_context(nc.sbuf_tensor(un("m8"), [128, 8], F32))
                tau = es.enter_context(nc.sbuf_tensor(un("tau"), [128, 1], F32))
                for i in range(NQT):
                    m = cx.rot("mk", 2)
                    P.dma("sp", mk[m][:], io["maskb"][i], "mk%d" % m, writes=["mk%d" % m])
                    P.emit("dve", lambda e, m=m: e.tensor_copy(out=acc[:], in_=mk[m][:]), reads=["mk%d" % m], writes=["acc"])
                    for h in range(IH):
                        for c in range(TT // 512):
                            pz = 4 + cx.rot("pb2", 2)
                            P.emit("pe", lambda e, pz=pz, h=h, i=i, c=c: e.matmul(cx.psum[pz][:], qiT[:, h, i * 128:(i + 1) * 128],
                                                                             kiT[:, c * 512:(c + 1) * 512], start=True, stop=True),
                                   reads=["qiT_%d" % h, "kiT"], writes=["ps%d" % pz])
                            r = cx.rot("rl", 2)
                            P.emit("act", lambda e, r=r, pz=pz: e.activation(out=rl[r][:], in_=cx.psum[pz][:], func=AF.Relu),
                                   reads=["ps%d" % pz], writes=["rl%d" % r])
                            P.emit("dve", lambda e, r=r, h=h, i=i, c=c: e.scalar_tensor_tensor(
                                out=acc[:, c * 512:(c + 1) * 512], in0=rl[r][:], scalar=wiT[:, i, h:h + 1], in1=acc[:, c * 512:(c + 1) * 512],
                                op0=ALU.mult, op1=ALU.add), reads=["rl%d" % r, "wiT", "acc"], writes=["acc"])
                    P.emit("act", lambda e: e.activation(out=work[:], in_=acc[:], func=AF.Copy), reads=["acc"], writes=["work"])
                    for rnd in range(TOPK // 8):
                        P.emit("dve", lambda e: e.max(out=m8[:], in_=work[:]), reads=["work"], writes=["m8"])
                        if rnd < TOPK // 8 - 1:
                            P.emit("dve", lambda e: e.match_replace(out=work[:], in_to_replace=m8[:], in_values=work[:], imm_value=3.0 * NEG),
                                   reads=["work", "m8"], writes=["work"])
                    P.emit("dve", lambda e: e.tensor_scalar(out=tau[:], in0=m8[:, 7:8], scalar1=0.5 * NEG, scalar2=None, op0=ALU.max),
                           reads=["m8"], writes=["tau"])
                    P.emit("dve", lambda e: e.tensor_scalar(out=work[:], in0=acc[:], scalar1=tau[:, 0:1], scalar2=-NEG, op0=ALU.is_ge, op1=ALU.mult),
                           reads=["acc", "tau", "work"], writes=["work"])
                    P.emit("dve", lambda e, i=i: e.tensor_scalar(out=masks[:, i, :], in0=work[:], scalar1=NEG, scalar2=None, op0=ALU.add),
                           reads=["work"], writes=["masks_%d" % i])
                P.barrier()
        KT = esP.enter_context(nc.sbuf_tensor(un("KT"), [128, DG, TT], BF16))
        V = esP.enter_context(nc.sbuf_tensor(un("V"), [128, TT // 128, DG * 128], BF16))
        with ExitStack() as es:
            cosK = es.enter_context(nc.sbuf_tensor(un("cosK"), [128, TT], F32))
            sinK = es.enter_context(nc.sbuf_tensor(un("sinK"), [128, TT], F32))
            cosQ = es.enter_context(nc.sbuf_tensor(un("cosQ"), [128, TOK], F32))
            sinQ = es.enter_context(nc.sbuf_tensor(un("sinQ"), [128, TOK], F32))
            emit_rope_tables(cx, io["pos_full"], TT, vecs[:, 2:3], vecs[:, 3:4], cosK, sinK, "tk")
            emit_rope_tables(cx, io["pos_own"], TOK, vecs[:, 2:3], vecs[:, 3:4], cosQ, sinQ, "tq")
            cx.wbuf = [es.enter_context(nc.sbuf_tensor(un("w%d" % i), [128, DC * 128], BF16)) for i in range(3)]
            cx.nw = 3
            alloc_rope_tmps(cx, es)
            hTc = es.enter_context(nc.sbuf_tensor(un("hTc"), [128, DC, 512], BF16))
            qst = [es.enter_context(nc.sbuf_tensor(un("qst%d" % i), [128, 512], BF16)) for i in range(2)]
            for tc in range(TT // 512):
                load_hTc(cx, hTc, io["hT_full"], tc * 512)
                for g in range(DG):
                    wt, wn = cx.load_w(io["wk"][g], DC * 128)
                    pz = 4 + cx.rot("pb2", 2)
                    emit_proj(cx, cx.psum[pz][:], "ps%d" % pz, DC, lambda kc, wt=wt, wn=wn: (wt[:, kc * 128:(kc + 1) * 128], wn),
                              lambda kc: (hTc[:, kc, :], "hTc"))
                    emit_rope(cx, cx.psum[pz][:], "ps%d" % pz, 128, 512, cosK[:, tc * 512:(tc + 1) * 512], sinK[:, tc * 512:(tc + 1) * 512],
                              ("tk_cos", "tk_sin"), perm128, KT[:, g, tc * 512:(tc + 1) * 512], "KT_%d" % g)
            for tq in range(TOK // 512):
                load_hTc(cx, hTc, io["hT_own"], tq * 512)
                for h in range(DH):
                    wt, wn = cx.load_w(io["wq"][h], DC * 128)
                    pz = 4 + cx.rot("pb2", 2)
                    emit_proj(cx, cx.psum[pz][:], "ps%d" % pz, DC, lambda kc, wt=wt, wn=wn: (wt[:, kc * 128:(kc + 1) * 128], wn),
                              lambda kc: (hTc[:, kc, :], "hTc"))
                    s = cx.rot("qst", 2)
                    emit_rope(cx, cx.psum[pz][:], "ps%d" % pz, 128, 512, cosQ[:, tq * 512:(tq + 1) * 512], sinQ[:, tq * 512:(tq + 1) * 512],
                              ("tq_cos", "tq_sin"), perm128, qst[s][:], "qst%d" % s)
                    P.dma("sp", qsv[h][:, tq * 512:(tq + 1) * 512], qst[s][:], "qst%d" % s, reads=["qst%d" % s], writes=["qscr_%d" % h])
            P.barrier()
        with ExitStack() as es:
            cx.wbuf = [es.enter_context(nc.sbuf_tensor(un("w%d" % i), [128, DC * 256], BF16)) for i in range(2)]
            cx.nw = 2
            hTc = es.enter_context(nc.sbuf_tensor(un("hTc"), [128, DC, 512], BF16))
            NV4 = (DG * 128) // 256
            ev = 0
            for tc in range(TT // 512):
                load_hTc(cx, hTc, io["hT_full"], tc * 512)
                for qv in range(NV4):
                    wt, wn = cx.load_w(io["wv"][qv], DC * 256)
                    for jj in range(2):
                        pz = 4 + cx.rot("pb2", 2)
                        for j2 in range(2):
                            j = jj * 2 + j2
                            emit_proj(cx, cx.psum[pz][:, j2 * 256:(j2 + 1) * 256], "ps%d" % pz, DC,
                                      lambda kc, j=j: (hTc[:, kc, j * 128:(j + 1) * 128], "hTc"),
                                      lambda kc, wt=wt, wn=wn: (wt[:, kc * 256:(kc + 1) * 256], wn))
                            emit_evac(cx, V[:, tc * 4 + j, qv * 256:(qv + 1) * 256], "V", cx.psum[pz][:, j2 * 256:(j2 + 1) * 256], "ps%d" % pz, ev)
                            ev += 1
            P.barrier()
        with ExitStack() as es:
            alloc_attn(cx, es, TT)
            qh = [es.enter_context(nc.sbuf_tensor(un("qh%d" % i), [128, TOK], BF16)) for i in range(2)]
            oth = [es.enter_context(nc.sbuf_tensor(un("oth%d" % i), [128, TOK], BF16)) for i in range(2)]
            ov = io["oT"].rearrange("(h p) t -> h p t", p=128)
            for h in range(DH):
                g = h // REP
                b = h % 2
                P.dma("sp", qh[b][:], qsv[h], "qh%d" % b, reads=["qscr_%d" % h], writes=["qh%d" % b])
                for i in range(NQT):
                    def pairs(c, i=i, b=b, g=g):
                        return [(qh[b][:, i * 128:(i + 1) * 128], "qh%d" % b, KT[:, g, c * 512:(c + 1) * 512], "KT_%d" % g)]
                    emit_attn_tile(cx, TT, pairs, lambda kb, g=g: (V[:, kb, g * 128:(g + 1) * 128], "V"), masks[:, i, :], "masks_%d" % i,
                                   scale, oth[b][:, i * 128:(i + 1) * 128], "oth%d" % b)
                P.dma("sp", ov[h], oth[b][:], "oth%d" % b, reads=["oth%d" % b], writes=["oT_%d" % h])
            P.barrier()


GN_EPS = 64e-5


def emit_rwkv(cx, io, RC, TT, GL, scan_split=True):
    P, nc, DC = cx.P, cx.nc, cx.DC
    RCB = RC // 128
    G = RCB
    NCH = TT // 512
    GLC = [(i * 128, min(128, GL - i * 128)) for i in range((GL + 127) // 128)]
    vecs = io["vecs"]
    vo = 6 * DC

    def vcol(k, cb):
        return vecs[:, vo + k * RCB + cb: vo + k * RCB + cb + 1]
    def scr(name, shape, dt=F32):
        return nc.dram_tensor(un(name), list(shape), dt).ap()
    rT, kT, vT, kpT, yT = (scr(n, [RC, TT]) for n in ("rT", "kT", "vT", "kpT", "yT"))
    gT = scr("gT", [RC, TT], BF16)
    tokm = scr("tokm", [TT, 5, 2, G * 64])
    bones = cx.cf[:, 1, :]
    with ExitStack() as esP:
        bo = esP.enter_context(nc.sbuf_tensor(un("bones"), [128, 128], F32))
        P.emit("dve", lambda e: e.memset(bo[:], 0.0), writes=["bones"])
        P.emit("dve", lambda e: e.memset(bo[0:64, 0:64], 1.0), reads=["bones"], writes=["bones"])
        P.emit("dve", lambda e: e.memset(bo[64:128, 64:128], 1.0), reads=["bones"], writes=["bones"])
        with ExitStack() as es:
            cx.wbuf = [es.enter_context(nc.sbuf_tensor(un("w%d" % i), [128, DC * 128], BF16)) for i in range(3)]
            cx.nw = 3
            hx = es.enter_context(nc.sbuf_tensor(un("hx"), [128, DC, 514], BF16))
            xx = es.enter_context(nc.sbuf_tensor(un("xx"), [128, DC, 512], BF16))
            xm = [es.enter_context(nc.sbuf_tensor(un("xm%d" % i), [128, DC, 512], BF16)) for i in range(1)]
            tw = es.enter_context(nc.sbuf_tensor(un("tw"), [128, 512], BF16))
            ta = es.enter_context(nc.sbuf_tensor(un("ta"), [128, 512], BF16))
            tg = es.enter_context(nc.sbuf_tensor(un("tg"), [128, len(GLC), 512], BF16))
            st3 = [es.enter_context(nc.sbuf_tensor(un("st3_%d" % i), [128, 512], F32)) for i in range(3)]
            NE = 12
            et = [[es.enter_context(nc.sbuf_tensor(un("et%d_%d" % (j, i)), [128, 512], F32)) for i in range(NE)] for j in range(2)]
            gst = [es.enter_context(nc.sbuf_tensor(un("gst%d" % i), [128, 512], BF16)) for i in range(2)]
            tst = [es.enter_context(nc.sbuf_tensor(un("tst%d" % i), [128, 512], F32)) for i in range(3)]
            hv = io["hT_full"].rearrange("(c p) t -> p c t", p=128)
            P.emit("dve", lambda e: e.memset(hx[:, :, 0:2], 0.0), writes=["hx"])
            evc = 0
            for tc in range(NCH):
                c0 = tc * 512
                if tc > 0:
                    P.emit("dve", lambda e: e.tensor_copy(out=hx[:, :, 1:2], in_=hx[:, :, 513:514]), reads=["hx"], writes=["hx"])
                P.dma("sp", hx[:, :, 2:514], hv[:, :, c0:c0 + 512], "hTc", reads=["hx"], writes=["hx"])
                for kc in range(DC):
                    P.emit("dve", lambda e, kc=kc: e.tensor_tensor(out=xx[:, kc, :], in0=hx[:, kc, 1:513], in1=hx[:, kc, 2:514], op=ALU.subtract),
                           reads=["hx"], writes=["xx_%d" % kc])

                def mix(mi):
                    s = cx.rot("xm", 1)
                    for kc in range(DC):
                        P.emit("dve", lambda e, kc=kc, s=s, mi=mi: e.scalar_tensor_tensor(
                            out=xm[s][:, kc, :], in0=xx[:, kc, :], scalar=vecs[:, mi * DC + kc: mi * DC + kc + 1], in1=hx[:, kc, 2:514],
                            op0=ALU.mult, op1=ALU.add), reads=["xx_%d" % kc, "hx", "vecs"], writes=["xm%d_%d" % (s, kc)])
                    return s

                def projfm(s, wslab, M, psap, psn):
                    wt, wn = cx.load_w(wslab, DC * M)
                    emit_proj(cx, psap, psn, DC, lambda kc: (wt[:, kc * M:(kc + 1) * M], wn), lambda kc: (xm[s][:, kc, :], "xm%d_%d" % (s, kc)))
                s = mix(1)
                pz = 4 + cx.rot("pb2", 2)
                projfm(s, io["w1"], 128, cx.psum[pz][:], "ps%d" % pz)
                P.emit("act", lambda e, pz=pz: e.activation(out=tw[:], in_=cx.psum[pz][:], func=AF.Tanh), reads=["ps%d" % pz], writes=["tw"])
                s = mix(4)
                pz = 4 + cx.rot("pb2", 2)
                projfm(s, io["a1"], 128, cx.psum[pz][:], "ps%d" % pz)
                P.emit("act", lambda e, pz=pz: e.activation(out=ta[:], in_=cx.psum[pz][:], func=AF.Copy), reads=["ps%d" % pz], writes=["ta"])
                s = mix(5)
                for gi, (g0, gm) in enumerate(GLC):
                    pz = 4 + cx.rot("pb2", 2)
                    projfm(s, io["g1"][gi][:, 0:DC * gm] if gm == 128 else io["g1b"], gm, cx.psum[pz][0:gm, :], "ps%d" % pz)
                    P.emit("act", lambda e, pz=pz, gi=gi, gm=gm: e.activation(out=tg[0:gm, gi, :], in_=cx.psum[pz][0:gm, :], func=AF.Sigmoid),
                           reads=["ps%d" % pz], writes=["tg"])
                for mi, wkey, dst in ((0, "wr", rT), (2, "wk", kT), (3, "wv", vT)):
                    s = mix(mi)
                    dv = dview(dst)
                    for cb in range(RCB):
                        pz = 4 + cx.rot("pb2", 2)
                        projfm(s, io[wkey][cb], 128, cx.psum[pz][:], "ps%d" % pz)
                        q = cx.rot("st3", 3)
                        emit_evac(cx, st3[q][:], "st3_%d" % q, cx.psum[pz][:], "ps%d" % pz, evc); evc += 1
                        P.dma("sp", dv[cb][:, c0:c0 + 512], st3[q][:], "st3_%d" % q, reads=["st3_%d" % q], writes=["%s_%d" % (wkey, cb)])
                for cb in range(RCB):
                    j = cx.rot("et", 2)
                    E = et[j]
                    en = ["et%d_%d" % (j, i) for i in range(NE)]
                    for i, (src, key) in enumerate(((rT, "wr"), (kT, "wk"), (vT, "wv"))):
                        P.dma("sp", E[i][:], dview(src)[cb][:, c0:c0 + 512], "etl%d_%d" % (j, i), reads=["%s_%d" % (key, cb)], writes=[en[i]])
                    r_, k_, v_ = E[0], E[1], E[2]
                    w2t, w2n = cx.load_w(io["w2"][cb], 128)
                    pz = 4 + cx.rot("pb2", 2)
                    P.emit("pe", lambda e, pz=pz, w2t=w2t: e.matmul(cx.psum[pz][:], w2t[:, 0:128], tw[:], start=True, stop=True),
                           reads=[w2n, "tw"], writes=["ps%d" % pz])
                    P.emit("act", lambda e, pz=pz, cb=cb, E=E: e.activation(out=E[3][:], in_=cx.psum[pz][:], func=AF.Sigmoid, bias=vcol(0, cb)),
                           reads=["ps%d" % pz, "vecs"], writes=[en[3]])
                    P.emit("act", lambda e, E=E: e.activation(out=E[4][:], in_=E[3][:], func=AF.Exp, scale=-math.exp(-0.5)),
                           reads=[en[3]], writes=[en[4]])
                    a2t, a2n = cx.load_w(io["a2"][cb], 128)
                    pz = 4 + cx.rot("pb2", 2)
                    P.emit("pe", lambda e, pz=pz, a2t=a2t: e.matmul(cx.psum[pz][:], a2t[:, 0:128], ta[:], start=True, stop=True),
                           reads=[a2n, "ta"], writes=["ps%d" % pz])
                    P.emit("act", lambda e, pz=pz, cb=cb, E=E: e.activation(out=E[5][:], in_=cx.psum[pz][:], func=AF.Sigmoid, bias=vcol(1, cb)),
                           reads=["ps%d" % pz, "vecs"], writes=[en[5]])
                    g2t, g2n = cx.load_w(io["g2"][cb], len(GLC) * 128)
                    pz = 4 + cx.rot("pb2", 2)
                    for gi, (g0, gm) in enumerate(GLC):
                        P.emit("pe", lambda e, pz=pz, g2t=g2t, gi=gi, gm=gm: e.matmul(cx.psum[pz][:], g2t[0:gm, gi * 128:(gi + 1) * 128], tg[0:gm, gi, :],
                                                                                start=(gi == 0), stop=(gi == len(GLC) - 1)),
                               reads=[g2n, "tg"], writes=["ps%d" % pz])
                    gq = cx.rot("gst", 2)
                    P.emit("act", lambda e, pz=pz, gq=gq: e.activation(out=gst[gq][:], in_=cx.psum[pz][:], func=AF.Copy), reads=["ps%d" % pz], writes=["gst%d" % gq])
                    P.dma("sp", dview(gT)[cb][:, c0:c0 + 512], gst[gq][:], "gst%d" % gq, reads=["gst%d" % gq], writes=["gT_%d" % cb])
                    P.emit("dve", lambda e, E=E, cb=cb: e.tensor_scalar(out=E[6][:], in0=E[1][:], scalar1=vcol(2, cb), scalar2=None, op0=ALU.mult),
                           reads=[en[1], "vecs"], writes=[en[6]])
                    P.emit("act", lambda e, E=E: e.activation(out=E[7][:], in_=E[6][:], func=AF.Square), reads=[en[6]], writes=[en[7]])
                    pz = 4 + cx.rot("pb2", 2)
                    P.emit("pe", lambda e, pz=pz, E=E: e.matmul(cx.psum[pz][:], bo[:], E[7][:], start=True, stop=True),
                           reads=["bones", en[7]], writes=["ps%d" % pz])
                    P.emit("act", lambda e, pz=pz, E=E: e.activation(out=E[7][:], in_=cx.psum[pz][:], func=AF.Sqrt), reads=["ps%d" % pz], writes=[en[7]])
                    P.emit("dve", lambda e, E=E: e.tensor_scalar(out=E[7][:], in0=E[7][:], scalar1=1e-12, scalar2=None, op0=ALU.max), reads=[en[7]], writes=[en[7]])
                    P.emit("dve", lambda e, E=E: e.reciprocal(out=E[7][:], in_=E[7][:]), reads=[en[7]], writes=[en[7]])
                    P.emit("dve", lambda e, E=E: e.scalar_tensor_tensor(out=E[8][:], in0=E[6][:], scalar=-1.0, in1=E[7][:], op0=ALU.mult, op1=ALU.mult),
                           reads=[en[6], en[7]], writes=[en[8]])
                    P.emit("dve", lambda e, E=E: e.scalar_tensor_tensor(out=E[9][:], in0=E[8][:], scalar=-1.0, in1=E[5][:], op0=ALU.mult, op1=ALU.mult),
                           reads=[en[8], en[5]], writes=[en[9]])
                    P.emit("dve", lambda e, E=E, cb=cb: e.tensor_scalar(out=E[10][:], in0=E[5][:], scalar1=-1.0, scalar2=vcol(3, cb), op0=ALU.add, op1=ALU.mult),
                           reads=[en[5], "vecs"], writes=[en[10]])
                    P.emit("dve", lambda e, E=E: e.scalar_tensor_tensor(out=E[10][:], in0=E[10][:], scalar=1.0, in1=E[1][:], op0=ALU.add, op1=ALU.mult),
                           reads=[en[10], en[1]], writes=[en[10]])
                    P.dma("sp", dview(kpT)[cb][:, c0:c0 + 512], E[10][:], "kps%d" % j, reads=[en[10]], writes=["kpT_%d" % cb])
                    for oi, src_i in enumerate((4, 8, 9, 10, 0)):
                        pz = 4 + cx.rot("pb2", 2)
                        for jt in range(4):
                            P.emit("pe", lambda e, pz=pz, jt=jt, E=E, src_i=src_i: e.transpose(
                                out=cx.psum[pz][:, jt * 128:(jt + 1) * 128], in_=E[src_i][:, jt * 128:(jt + 1) * 128], identity=cx.cf[:, 0, :]),
                                reads=[en[src_i], "cf"], writes=["ps%d" % pz])
                        tq = cx.rot("tst", 3)
                        emit_evac(cx, tst[tq][:], "tst%d" % tq, cx.psum[pz][:], "ps%d" % pz, evc); evc += 1
                        for jt in range(4):
                            dstv = tokm[c0 + jt * 128:c0 + (jt + 1) * 128, oi, :, cb * 64:(cb + 1) * 64]
                            P.dma("sp", dstv, tst[tq][:, jt * 128:(jt + 1) * 128].rearrange("p (h k) -> p h k", h=2), "tst%d" % tq,
                                  reads=["tst%d" % tq], writes=["tokm_%d_%d_%d_%d" % (tc, cb, oi, jt)], grp=("tk", tc, cb, oi))
            P.barrier()
        with ExitStack() as es:
            TC = 2
            bc = [es.enter_context(nc.sbuf_tensor(un("bc%d" % i), [128, TC, 5, G * 64], F32)) for i in range(2)]
            S = [es.enter_context(nc.sbuf_tensor(un("S%d" % i), [128, G, 64], F32)) for i in range(2)]
            tmp = es.enter_context(nc.sbuf_tensor(un("tmp"), [128, G, 64], F32))
            tmp2s = [es.enter_context(nc.sbuf_tensor(un("tmp2_%d" % i), [128, G, 64], F32)) for i in range(2)]
            tmp3 = es.enter_context(nc.sbuf_tensor(un("tmp3"), [128, G, 64], F32))
            sa = es.enter_context(nc.sbuf_tensor(un("sa"), [128, G], F32))
            vch = [es.enter_context(nc.sbuf_tensor(un("vch%d" % i), [128, G, 512], F32)) for i in range(1)]
            ych = [es.enter_context(nc.sbuf_tensor(un("ych%d" % i), [128, G, 512], F32)) for i in range(1)]
            P.emit("dve", lambda e: e.memset(S[0][:], 0.0), writes=["S0"])
            e2 = "pool" if scan_split else "dve"
            vv = vT.rearrange("(g p) t -> p g t", p=128)
            yv = yT.rearrange("(g p) t -> p g t", p=128)
            for tb in range(TT // 512):
                P.dma("sp", vch[0][:], vv[:, :, tb * 512:(tb + 1) * 512], "vch", reads=["wv_%d" % i for i in range(RCB)], writes=["vch"])
                for t2 in range(512 // TC):
                    t0 = tb * 512 + t2 * TC
                    bq = cx.rot("bc", 2)
                    for hp in range(2):
                        P.dma("sp", bc[bq][hp * 64:(hp + 1) * 64], tokm[t0:t0 + TC, :, hp, :].partition_broadcast(64), "bc%d_%d" % (bq, hp),
                              writes=["bc%d_%d" % (bq, hp)])
                    for j in range(TC):
                        tl = t2 * TC + j
                        t = t0 + j
                        s0, s1 = S[t % 2], S[(t + 1) % 2]
                        n0, n1 = "S%d" % (t % 2), "S%d" % ((t + 1) % 2)
                        Wv, Av, Bv, Kv, Rv = (bc[bq][:, j, oi, :].rearrange("p (g k) -> p g k", k=64) for oi in range(5))
                        bn = "bc%d_0" % bq
                        bn1 = "bc%d_1" % bq
                        tmp2 = tmp2s[t % 2]
                        t2n = "tmp2_%d" % (t % 2)
                        P.emit(e2, lambda e, Kv=Kv, tl=tl, tmp2=tmp2: e.tensor_tensor(out=tmp2[:], in0=Kv, in1=vch[0][:, :, tl:tl + 1].to_broadcast([128, G, 64]), op=ALU.mult),
                               reads=[bn, bn1, "vch"], writes=[t2n])
                        P.emit("dve", lambda e, s0=s0, Av=Av: e.tensor_tensor(out=tmp[:], in0=s0[:], in1=Av, op=ALU.mult), reads=[n0, bn, bn1], writes=["tmp"])
                        P.emit("dve", lambda e: e.tensor_reduce(out=sa[:], in_=tmp[:], axis=AX.X, op=ALU.add), reads=["tmp"], writes=["sa"])
                        P.emit("dve", lambda e, s0=s0, s1=s1, Wv=Wv: e.tensor_tensor(out=s1[:], in0=s0[:], in1=Wv, op=ALU.mult), reads=[n0, bn, bn1], writes=[n1])
                        P.emit("dve", lambda e, Bv=Bv: e.tensor_tensor(out=tmp[:], in0=Bv, in1=sa[:].unsqueeze(2).to_broadcast([128, G, 64]), op=ALU.mult),
                               reads=[bn, bn1, "sa", "tmp"], writes=["tmp"])
                        P.emit("dve", lambda e, s1=s1: e.tensor_tensor(out=s1[:], in0=s1[:], in1=tmp[:], op=ALU.add), reads=[n1, "tmp"], writes=[n1])
                        P.emit("dve", lambda e, s1=s1, tmp2=tmp2: e.tensor_tensor(out=s1[:], in0=s1[:], in1=tmp2[:], op=ALU.add), reads=[n1, t2n], writes=[n1])
                        P.emit("dve", lambda e, s1=s1, Rv=Rv: e.tensor_tensor(out=tmp3[:], in0=s1[:], in1=Rv, op=ALU.mult), reads=[n1, bn, bn1], writes=["tmp3"])
                        P.emit("dve", lambda e, tl=tl: e.tensor_reduce(out=ych[0][:, :, tl], in_=tmp3[:], axis=AX.X, op=ALU.add), reads=["tmp3"], writes=["ych"])
                P.dma("sp", yv[:, :, tb * 512:(tb + 1) * 512], ych[0][:], "ycho", reads=["ych"], writes=["yT"])
            P.barrier()
        with ExitStack() as es:
            NE = 8
            et = [[es.enter_context(nc.sbuf_tensor(un("dt%d_%d" % (j, i)), [128, 512], F32)) for i in range(NE)] for j in range(2)]
            gin = [es.enter_context(nc.sbuf_tensor(un("gin%d" % i), [128, 512], BF16)) for i in range(2)]
            yo = [es.enter_context(nc.sbuf_tensor(un("yo%d" % i), [128, 512], BF16)) for i in range(2)]
            gneps = es.enter_context(nc.sbuf_tensor(un("gneps"), [128, 1], F32))
            P.emit("dve", lambda e: e.memset(gneps[:], GN_EPS), writes=["gneps"])
            ov = dview(io["ygT"])
            for tc in range(NCH):
                c0 = tc * 512
                for cb in range(RCB):
                    j = cx.rot("dt", 2)
                    E = et[j]
                    en = ["dt%d_%d" % (j, i) for i in range(NE)]
                    for i, src in enumerate((yT, rT, kpT, vT)):
                        P.dma("sp", E[i][:], dview(src)[cb][:, c0:c0 + 512], "dtl%d_%d" % (j, i), writes=[en[i]])
                    P.dma("sp", gin[j][:], dview(gT)[cb][:, c0:c0 + 512], "gin%d" % j, writes=["gin%d" % j])
                    pz = 4 + cx.rot("pb2", 2)
                    P.emit("pe", lambda e, pz=pz, E=E: e.matmul(cx.psum[pz][:], bo[:], E[0][:], start=True, stop=True), reads=["bones", en[0]], writes=["ps%d" % pz])
                    P.emit("dve", lambda e, pz=pz, E=E: e.scalar_tensor_tensor(out=E[4][:], in0=cx.psum[pz][:], scalar=-1.0 / 64, in1=E[0][:], op0=ALU.mult, op1=ALU.add),
                           reads=["ps%d" % pz, en[0]], writes=[en[4]])
                    P.emit("act", lambda e, E=E: e.activation(out=E[5][:], in_=E[4][:], func=AF.Square), reads=[en[4]], writes=[en[5]])
                    pz = 4 + cx.rot("pb2", 2)
                    P.emit("pe", lambda e, pz=pz, E=E: e.matmul(cx.psum[pz][:], bo[:], E[5][:], start=True, stop=True), reads=["bones", en[5]], writes=["ps%d" % pz])
                    P.emit("act", lambda e, pz=pz, E=E: e.activation(out=E[5][:], in_=cx.psum[pz][:], func=AF.Sqrt, scale=1.0 / 64, bias=gneps[:, 0:1]),
                           reads=["ps%d" % pz, "gneps"], writes=[en[5]])
                    P.emit("dve", lambda e, E=E: e.reciprocal(out=E[5][:], in_=E[5][:]), reads=[en[5]], writes=[en[5]])
                    P.emit("dve", lambda e, E=E: e.tensor_tensor(out=E[4][:], in0=E[4][:], in1=E[5][:], op=ALU.mult), reads=[en[4], en[5]], writes=[en[4]])
                    P.emit("dve", lambda e, E=E, cb=cb: e.tensor_scalar(out=E[4][:], in0=E[4][:], scalar1=vcol(4, cb), scalar2=vcol(5, cb), op0=ALU.mult, op1=ALU.add),
                           reads=[en[4], "vecs"], writes=[en[4]])
                    P.emit("dve", lambda e, E=E, cb=cb: e.scalar_tensor_tensor(out=E[6][:], in0=E[1][:], scalar=vcol(6, cb), in1=E[2][:], op0=ALU.mult, op1=ALU.mult),
                           reads=[en[1], en[2], "vecs"], writes=[en[6]])
                    pz = 4 + cx.rot("pb2", 2)
                    P.emit("pe", lambda e, pz=pz, E=E: e.matmul(cx.psum[pz][:], bo[:], E[6][:], start=True, stop=True), reads=["bones", en[6]], writes=["ps%d" % pz])
                    P.emit("dve", lambda e, pz=pz, E=E: e.tensor_tensor(out=E[6][:], in0=cx.psum[pz][:], in1=E[3][:], op=ALU.mult), reads=["ps%d" % pz, en[3]], writes=[en[6]])
                    P.emit("dve", lambda e, E=E: e.tensor_tensor(out=E[4][:], in0=E[4][:], in1=E[6][:], op=ALU.add), reads=[en[4], en[6]], writes=[en[4]])
                    P.emit("dve", lambda e, E=E, j=j: e.tensor_tensor(out=yo[j][:], in0=E[4][:], in1=gin[j][:], op=ALU.mult), reads=[en[4], "gin%d" % j], writes=["yo%d" % j])
                    P.dma("sp", ov[cb][:, c0:c0 + 512], yo[j][:], "yo%d" % j, reads=["yo%d" % j], writes=["yg_%d" % cb])
            P.barrier()


class Cfg:
    D = 4096; F = 6144; B = 4; SEQ = 2048; NSPLIT = 2
    MH = 32; QL = 1024; KVL = 512
    DH = 32; DG = 8; IH = 32; TOPK = 256
    GL = 480


CFG = Cfg()


def emit_tail(cx, cfg, x_in, xkey, o_src, KO, wo, ffns, vecs, norm_out, final, dests):
    P, nc, TOK, DC = cx.P, cx.nc, cx.TOK, cx.DC
    cur, ckey = x_in, xkey
    si = 0
    if o_src is not None:
        with ExitStack() as es:
            cx.basics(es)
            oT = es.enter_context(nc.sbuf_tensor(un("oTs"), [128, KO, TOK], BF16))
            P.dma("sp", oT[:], o_src.rearrange("(h p) t -> p h t", p=128), "oTs", writes=["oTs_%d" % k for k in range(KO)])
            dst = dests[si]; si += 1
            emit_down(cx, oT, "oTs", KO, lambda i: wo[i], cur, ckey, dst[0], dst[1], 1.0)
            cur, ckey = dst
            P.barrier()
    for (wgu, wd, gcol) in ffns:
        with ExitStack() as es:
            cx.basics(es)
            hT = es.enter_context(nc.sbuf_tensor(un("hT"), [128, DC, TOK], BF16))
            aT = es.enter_context(nc.sbuf_tensor(un("aT"), [128, cfg.F // 128 // cfg.NSPLIT, TOK], BF16))
            dst = dests[si]; si += 1
            emit_ffn(cx, cur, ckey, dst[0], dst[1], wgu, wd, vecs[:, gcol:gcol + DC], "vecs", hT, aT, cfg.F, cfg.NSPLIT)
            cur, ckey = dst
            P.barrier()
    with ExitStack() as es:
        cx.basics(es)
        ap, gcol = norm_out
        emit_norm(cx, cur, ckey, vecs[:, gcol:gcol + DC], "vecs", out_dram=ap, out_dt=(F32 if final else BF16), okey="nout")
        P.barrier()
    return cur, ckey


def build_program(cfg, kind):
    nc = bass.Bass("TRN2", target_bir_lowering=False)
    D, F, TOK, TT = cfg.D, cfg.F, cfg.SEQ // 2, cfg.SEQ
    DC, FB = D // 128, F // 128
    FH = FB // cfg.NSPLIT

    def inp(name, shape, dt=F32):
        return nc.dram_tensor(name, list(shape), dt, kind="ExternalInput").ap()

    def outp(name, shape, dt=F32):
        return nc.dram_tensor(name, list(shape), dt, kind="ExternalOutput").ap()

    consts = inp("consts", [128, 3, 128])
    cx = Ctx(nc, D, TOK, consts)
    P = cx.P

    def ffn_in(tag):
        return (inp("wgu" + tag, [FB, 2, 128, DC * 128]), inp("wd" + tag, [cfg.NSPLIT, DC, 128, FH * 128]))

    def load_vecs(nv):
        vd = inp("vecs_in", [128, nv])
        v = nc.alloc_sbuf_tensor("vecs", [128, nv], F32)
        P.dma("sp", v[:], vd, "vecs", writes=["vecs"])
        return v
    scr = [(nc.dram_tensor("xs%d" % i, [D, TOK], F32).ap(), "xs%d" % i) for i in range(2)]
    if kind == "first":
        x = inp("xT", [D, TOK])
        vecs = load_vecs(2 * DC)
        f = ffn_in("A")
        xo = outp("xo", [D, TOK])
        ho = outp("ho", [D, TOK], BF16)
        emit_tail(cx, cfg, x, "xin", None, 0, None, [(f[0], f[1], 0)], vecs, (ho, DC), False, [(xo, "xo")])
    elif kind in ("mla", "mla_last", "dsa"):
        x = inp("xT", [D, TOK])
        io = dict(hT_full=inp("hT_full", [D, TT], BF16), hT_own=inp("hT_own", [D, TOK], BF16),
                  pos_full=inp("pos_full", [1, TT], I32), pos_own=inp("pos_own", [1, TOK], I32),
                  maskb=inp("maskb", [TOK // 128, 128, TT], BF16))
        nff = 1 if kind == "mla_last" else 2
        if kind == "dsa":
            DH, DG, IH = cfg.DH, cfg.DG, cfg.IH
            io.update(wq=inp("wq", [DH, 128, DC * 128]), wk=inp("wk", [DG, 128, DC * 128]), wv=inp("wv", [DG * 128 // 256, 128, DC * 256]),
                      wqi=inp("wqi", [IH, 128, DC * 128]), wki=inp("wki", [128, DC * 128]), wwi=inp("wwi", [128, DC * IH]))
            nmv = 4
            KO = DH
        else:
            QC, KVC = cfg.QL // 128, cfg.KVL // 128
            io.update(win_q=inp("win_q", [QC, 128, DC * 128]), win_kv=inp("win_kv", [KVC, 128, DC * 128]), win_kr=inp("win_kr", [128, DC * 64]),
                      wuq=inp("wuq", [cfg.MH, 128, QC * 192]), wukv=inp("wukv", [cfg.MH, 128, KVC * 256]))
            nmv = 4 + QC + KVC
            KO = cfg.MH
        vecs = load_vecs(nmv + (nff + 1) * DC)
        io["vecs"] = vecs
        wo = inp("wo", [DC, 128, KO * 128])
        ffs = [ffn_in(t) for t in ("A", "B")[:nff]]
        oscr = nc.dram_tensor("oscr", [KO * 128, TOK], BF16).ap()
        io["oT"] = oscr
        if kind == "dsa":
            emit_dsa(cx, io, cfg.DH, cfg.DG, cfg.IH, cfg.TOPK, TT)
        else:
            emit_mla(cx, io, cfg.MH, cfg.QL, cfg.KVL, TT)
        xo = outp("xo", [D, TOK])
        if kind == "mla_last":
            emit_tail(cx, cfg, x, "xin", oscr, KO, wo, [(ffs[0][0], ffs[0][1], nmv)], vecs, (xo, nmv + DC), True, [scr[0], scr[1]])
        else:
            ho = outp("ho", [D, TOK], BF16)
            emit_tail(cx, cfg, x, "xin", oscr, KO, wo, [(ffs[0][0], ffs[0][1], nmv), (ffs[1][0], ffs[1][1], nmv + DC)], vecs,
                      (ho, nmv + 2 * DC), False, [scr[0], scr[1], (xo, "xo")])
    elif kind == "rwkv":
        RC = D // 2
        RCB = RC // 128
        GL = cfg.GL
        ng = (GL + 127) // 128
        io = dict(hT_full=inp("hT_full", [D, TT], BF16))
        for k in ("wr", "wk", "wv"):
            io[k] = inp(k, [RCB, 128, DC * 128])
        io["w1"] = inp("w1", [128, DC * 128]); io["a1"] = inp("a1", [128, DC * 128])
        io["w2"] = inp("w2", [RCB, 128, 128]); io["a2"] = inp("a2", [RCB, 128, 128])
        io["g1"] = inp("g1", [GL // 128, 128, DC * 128]); io["g1b"] = inp("g1b", [128, DC * (GL % 128)])
        io["g2"] = inp("g2", [RCB, 128, ng * 128])
        io["vecs"] = load_vecs(6 * DC + 7 * RCB)
        io["ygT"] = outp("ygT", [RC, TT], BF16)
        cxr = cx
        cxr.TOK, cxr.NT = 512, 1
        emit_rwkv(cx, io, RC, TT, GL, scan_split=True)
    elif kind == "post":
        x = inp("xT", [D, TOK])
        yg = inp("yg", [D, TOK], BF16)
        vecs = load_vecs(3 * DC)
        wo = inp("wo", [DC, 128, DC * 128])
        ffs = [ffn_in("A"), ffn_in("B")]
        xo = outp("xo", [D, TOK])
        ho = outp("ho", [D, TOK], BF16)
        emit_tail(cx, cfg, x, "xin", yg, DC, wo, [(ffs[0][0], ffs[0][1], 0), (ffs[1][0], ffs[1][1], DC)], vecs, (ho, 2 * DC), False, [scr[0], scr[1], (xo, "xo")])
    P.barrier()
    cx.stats = P.finalize()
    return nc


def ffn_tiles(cfg, w_gu, w_down):
    D, F = cfg.D, cfg.F
    DC, FB = D // 128, F // 128
    FH = FB // cfg.NSPLIT
    a = w_gu.reshape(DC, 128, 2, FB, 128)
    wgu = np.ascontiguousarray(a.transpose(3, 2, 1, 0, 4)).reshape(FB, 2, 128, DC * 128)
    b = w_down.reshape(cfg.NSPLIT, FH, 128, DC, 128)
    wd = np.ascontiguousarray(b.transpose(0, 3, 2, 1, 4)).reshape(cfg.NSPLIT, DC, 128, FH * 128)
    return wgu, wd


_PROGS = {}


def get_prog(cfg, kind):
    key = (id(cfg), kind)
    if key not in _PROGS:
        _PROGS[key] = build_program(cfg, kind)
    return _PROGS[key]


def launch(cfg, kind, shared, percore):
    import time as _t
    t0 = _t.time()
    nc = get_prog(cfg, kind)
    print("[kernel] launch %s: built %.1fs" % (kind, _t.time() - t0), flush=True)
    n = len(percore)
    in_maps = []
    for c in range(n):
        m = dict(shared)
        m.update(percore[c])
        in_maps.append(m)
    res = run_bass_kernel_spmd(nc, in_maps, core_ids=list(range(n)))
    print("[kernel] launch %s: done %.1fs" % (kind, _t.time() - t0), flush=True)
    return res.results


def run_model(cfg, inp):
    D, F, B, SEQ = cfg.D, cfg.F, cfg.B, cfg.SEQ
    TOK, TT = SEQ // 2, SEQ
    NCORE = 2 * B
    f32 = lambda a: np.ascontiguousarray(np.asarray(a, dtype=np.float32))
    x = f32(inp["x"])
    pos = np.ascontiguousarray(np.asarray(inp["positions"], dtype=np.int32))
    consts = const_tables()
    rv = rope_vecs()
    core = [(c // 2, c % 2) for c in range(NCORE)]
    masks = [causal_maskb(h, TOK, TT) for h in range(2)]

    def ffn(i, j, tag):
        wgu, wd = ffn_tiles(cfg, f32(inp["ffn_w_gu_%d" % i][j]), f32(inp["ffn_w_down_%d" % i][j]))
        return {"wgu" + tag: wgu, "wd" + tag: wd}

    def fnorm(i, j):
        return vec_cols(f32(inp["ffn_norm_%d" % i][j]))

    def mnorm(i):
        return vec_cols(f32(inp["mix_norm_%d" % i]))

    def full_h(ho):
        return [np.ascontiguousarray(np.concatenate([ho[2 * b], ho[2 * b + 1]], axis=1)) for b in range(B)]

    def attn_percore(xo, ho):
        hf = full_h(ho)
        return [dict(xT=xo[c], hT_full=hf[b], hT_own=ho[c], pos_full=pos[b][None, :], pos_own=np.ascontiguousarray(pos[b][None, h * TOK:(h + 1) * TOK]),
                     maskb=masks[h]) for c, (b, h) in enumerate(core)]

    def mla_w(p):
        w_in, w_uq, w_ukv, w_o = (f32(inp[p + k]) for k in ("w_in", "w_uq", "w_ukv", "w_o"))
        QL, KVL, MH = cfg.QL, cfg.KVL, cfg.MH
        return dict(win_q=tile_cols(w_in, np.arange(QL), 128), win_kv=tile_cols(w_in, QL + np.arange(KVL), 128),
                    win_kr=tile_cols(w_in, QL + KVL + np.arange(64), 64)[0], wuq=tile_cols(w_uq, np.arange(MH * 192), 192),
                    wukv=tile_cols(w_ukv, np.arange(MH * 256), 256), wo=tile_cols(w_o, np.arange(D), 128))

    sh = dict(consts=consts, vecs_in=np.concatenate([fnorm(0, 0), mnorm(0)], 1))
    sh.update(ffn(0, 0, "A"))
    pc = [dict(xT=np.ascontiguousarray(x[b, h * TOK:(h + 1) * TOK, :].T)) for (b, h) in core]
    r = launch(cfg, "first", sh, pc)
    xo = [np.asarray(q["xo"]) for q in r]
    ho = [np.asarray(q["ho"]) for q in r]
    del sh
    sh = dict(consts=consts, vecs_in=np.concatenate([rv, vec_cols(f32(inp["mla0_q_norm"])), vec_cols(f32(inp["mla0_kv_norm"])),
                                                     fnorm(0, 1), fnorm(1, 0), mnorm(1)], 1))
    sh.update(mla_w("mla0_")); sh.update(ffn(0, 1, "A")); sh.update(ffn(1, 0, "B"))
    r = launch(cfg, "mla", sh, attn_percore(xo, ho))
    xo = [np.asarray(q["xo"]) for q in r]
    ho = [np.asarray(q["ho"]) for q in r]
    del sh
    w_in = f32(inp["dsa1_w_in"])
    DH, DG, IH = cfg.DH, cfg.DG, cfg.IH
    o0 = 0
    cq = np.arange(DH * 128); o0 += DH * 128
    ck = o0 + np.arange(DG * 128); o0 += DG * 128
    cv = o0 + np.arange(DG * 128); o0 += DG * 128
    cqi = o0 + np.arange(IH * 128); o0 += IH * 128
    cki = o0 + np.arange(128); o0 += 128
    cwi = o0 + np.arange(IH)
    sh = dict(consts=consts, vecs_in=np.concatenate([rv, fnorm(1, 1), fnorm(2, 0), mnorm(2)], 1),
              wq=tile_cols(w_in, cq, 128), wk=tile_cols(w_in, ck, 128), wv=tile_cols(w_in, cv, 256),
              wqi=tile_cols(w_in, cqi, 128), wki=tile_cols(w_in, cki, 128)[0], wwi=tile_cols(w_in, cwi, IH)[0],
              wo=tile_cols(f32(inp["dsa1_w_o"]), np.arange(D), 128))
    sh.update(ffn(1, 1, "A")); sh.update(ffn(2, 0, "B"))
    r = launch(cfg, "dsa", sh, attn_percore(xo, ho))
    xo = [np.asarray(q["xo"]) for q in r]
    ho = [np.asarray(q["ho"]) for q in r]
    del sh, w_in
    RC = D // 2
    RCB = RC // 128
    GL = cfg.GL
    ng = (GL + 127) // 128
    g = lambda k: f32(inp["rwkv2_" + k])
    mu = g("mu")
    g2 = g("g2")
    g2p = np.zeros((ng * 128, D), np.float32)
    g2p[:GL] = g2
    sh = dict(consts=consts, w1=tile_cols(g("w1"), np.arange(128), 128)[0], a1=tile_cols(g("a1"), np.arange(128), 128)[0],
              g1=tile_cols(g("g1"), np.arange((GL // 128) * 128), 128), g1b=tile_cols(g("g1"), (GL // 128) * 128 + np.arange(GL % 128), GL % 128)[0])
    hf = full_h(ho)
    halves = []
    for hh in range(2):
        own = hh * RC + np.arange(RC)
        vec = np.concatenate([vec_cols(mu[i]) for i in range(6)] +
                             [vec_cols(g(k)[own]) for k in ("w0", "a0", "k_k", "k_a", "lnx_w", "lnx_b")] + [vec_cols(g("r_k").reshape(-1)[own])], 1)
        halves.append(dict(wr=tile_cols(g("w_r"), own, 128), wk=tile_cols(g("w_k"), own, 128), wv=tile_cols(g("w_v"), own, 128),
                           w2=tile_cols(g("w2"), own, 128), a2=tile_cols(g("a2"), own, 128),
                           g2=np.ascontiguousarray(g2p[:, own].reshape(ng, 128, RCB, 128).transpose(2, 1, 0, 3)).reshape(RCB, 128, ng * 128),
                           vecs_in=vec))
    pc = []
    for c, (b, h) in enumerate(core):
        m = dict(halves[h]); m["hT_full"] = hf[b]
        pc.append(m)
    r = launch(cfg, "rwkv", sh, pc)
    yg = [np.asarray(q["ygT"]) for q in r]
    del sh, halves, pc
    ygf = [np.concatenate([yg[2 * b], yg[2 * b + 1]], axis=0) for b in range(B)]
    sh = dict(consts=consts, vecs_in=np.concatenate([fnorm(2, 1), fnorm(3, 0), mnorm(3)], 1), wo=tile_cols(g("w_o"), np.arange(D), 128))
    sh.update(ffn(2, 1, "A")); sh.update(ffn(3, 0, "B"))
    pc = [dict(xT=xo[c], yg=np.ascontiguousarray(ygf[b][:, h * TOK:(h + 1) * TOK])) for c, (b, h) in enumerate(core)]
    r = launch(cfg, "post", sh, pc)
    xo = [np.asarray(q["xo"]) for q in r]
    ho = [np.asarray(q["ho"]) for q in r]
    del sh
    sh = dict(consts=consts, vecs_in=np.concatenate([rv, vec_cols(f32(inp["mla3_q_norm"])), vec_cols(f32(inp["mla3_kv_norm"])),
                                                     fnorm(3, 1), vec_cols(f32(inp["final_norm"]))], 1))
    sh.update(mla_w("mla3_")); sh.update(ffn(3, 1, "A"))
    r = launch(cfg, "mla_last", sh, attn_percore(xo, ho))
    out = np.empty((B, SEQ, D), np.float32)
    for c, (b, h) in enumerate(core):
        out[b, h * TOK:(h + 1) * TOK, :] = np.asarray(r[c]["xo"]).T
    return out


INPUT_NAMES = (
    "x",
    "positions",
    "ffn_norm_0",
    "ffn_w_gu_0",
    "ffn_w_down_0",
    "mix_norm_0",
    "mla0_w_in",
    "mla0_q_norm",
    "mla0_w_uq",
    "mla0_kv_norm",
    "mla0_w_ukv",
    "mla0_w_o",
    "ffn_norm_1",
    "ffn_w_gu_1",
    "ffn_w_down_1",
    "mix_norm_1",
    "dsa1_w_in",
    "dsa1_w_o",
    "ffn_norm_2",
    "ffn_w_gu_2",
    "ffn_w_down_2",
    "mix_norm_2",
    "rwkv2_mu",
    "rwkv2_w_r",
    "rwkv2_w_k",
    "rwkv2_w_v",
    "rwkv2_w_o",
    "rwkv2_w0",
    "rwkv2_w1",
    "rwkv2_w2",
    "rwkv2_a0",
    "rwkv2_a1",
    "rwkv2_a2",
    "rwkv2_g1",
    "rwkv2_g2",
    "rwkv2_k_k",
    "rwkv2_k_a",
    "rwkv2_r_k",
    "rwkv2_lnx_w",
    "rwkv2_lnx_b",
    "ffn_norm_3",
    "ffn_w_gu_3",
    "ffn_w_down_3",
    "mix_norm_3",
    "mla3_w_in",
    "mla3_q_norm",
    "mla3_w_uq",
    "mla3_kv_norm",
    "mla3_w_ukv",
    "mla3_w_o",
    "final_norm",
)


def kernel(**inputs):
    missing = [n for n in INPUT_NAMES if n not in inputs]
    assert not missing, missing
    return run_model(CFG, inputs)
```
